# Optimizing a Trainium2 kernel written in Bass

```python
import jax, jax.numpy as jnp
from jax import lax
import numpy as np

D_MODEL = 1024
BATCH = 8
SEQ = 2048
DEPTH = 4

FOX_HEAD_DIM = 64
FOX_WIDTH = D_MODEL // 2
N_FOX_HEADS = FOX_WIDTH // FOX_HEAD_DIM
CONV_WIDTH = D_MODEL // 2
CONV_KERNEL = 3
N_MEM_HEADS = 4
MEM_HEAD_DIM = 128
MEM_WIDTH = N_MEM_HEADS * MEM_HEAD_DIM
MEM_TOKENS = 256
D_FF = -(-(8 * D_MODEL) // (3 * 256)) * 256
Q_BLOCK = 128
EPS = 1e-6
IN_SPLITS = (FOX_WIDTH, FOX_WIDTH, FOX_WIDTH, N_FOX_HEADS, CONV_WIDTH, CONV_WIDTH, CONV_WIDTH, D_MODEL, D_MODEL)
IN_WIDTH = sum(IN_SPLITS)

kernel_name = "fox_shortconv_gated_hybrid"


def rmsnorm(x, g):
    xf = x.astype(jnp.float32)
    y = xf * lax.rsqrt(jnp.mean(xf * xf, axis=-1, keepdims=True) + EPS)
    return (y * g.astype(jnp.float32)).astype(x.dtype)


def split_cols(p, sizes):
    out, start = [], 0
    for s in sizes:
        out.append(p[..., start:start + s])
        start += s
    return out


def forgetting_attention(q, k, v, log_f):
    b, t, h, hd = q.shape
    q = q.transpose(0, 2, 1, 3)
    k = k.transpose(0, 2, 1, 3)
    v = v.transpose(0, 2, 1, 3)
    c = jnp.cumsum(log_f, axis=1).transpose(0, 2, 1)
    scale = hd ** -0.5
    outs = []
    for i in range(t // Q_BLOCK):
        qs, ke = i * Q_BLOCK, (i + 1) * Q_BLOCK
        s = jnp.einsum('bhqd,bhkd->bhqk', q[:, :, qs:ke], k[:, :, :ke],
                       preferred_element_type=jnp.float32) * scale
        s = s + (c[:, :, qs:ke, None] - c[:, :, None, :ke])
        mask = (qs + jnp.arange(Q_BLOCK))[:, None] >= jnp.arange(ke)[None, :]
        p = jax.nn.softmax(jnp.where(mask, s, -jnp.inf), axis=-1).astype(v.dtype)
        outs.append(jnp.einsum('bhqk,bhkd->bhqd', p, v[:, :, :ke]))
    o = jnp.concatenate(outs, axis=2)
    return o.transpose(0, 2, 1, 3).reshape(b, t, h * hd)


def causal_depthwise_conv(z, w):
    ch = z.shape[-1]
    return lax.conv_general_dilated(
        z, w.astype(z.dtype)[:, None, :], window_strides=(1,),
        padding=[(CONV_KERNEL - 1, 0)], dimension_numbers=('NWC', 'WIO', 'NWC'),
        feature_group_count=ch)


def hybrid_mixer(u, w_in, b_f, q_gain, k_gain, conv_w, w_up_a, w_up_b, w_o):
    b, t, _ = u.shape
    p = u @ w_in
    q, k, v, f_logit, z, gate_b_in, gate_c_in, g_a, g_b = split_cols(p, IN_SPLITS)
    q = rmsnorm(q.reshape(b, t, N_FOX_HEADS, FOX_HEAD_DIM), q_gain)
    k = rmsnorm(k.reshape(b, t, N_FOX_HEADS, FOX_HEAD_DIM), k_gain)
    v = v.reshape(b, t, N_FOX_HEADS, FOX_HEAD_DIM)
    log_f = jax.nn.log_sigmoid(f_logit.astype(jnp.float32) + b_f.astype(jnp.float32))
    a_out = forgetting_attention(q, k, v, log_f)
    c_out = gate_b_in * causal_depthwise_conv(gate_c_in * z, conv_w)
    merged = jax.nn.sigmoid(g_a) * (a_out @ w_up_a) + jax.nn.sigmoid(g_b) * (c_out @ w_up_b)
    return merged @ w_o


def memory_attention(u, mem_n, w_cq, w_ckv, cq_gain, ck_gain, w_co):
    b, t, _ = u.shape
    m = mem_n.shape[1]
    q = rmsnorm((u @ w_cq).reshape(b, t, N_MEM_HEADS, MEM_HEAD_DIM), cq_gain)
    kv = mem_n @ w_ckv
    k = rmsnorm(kv[..., :MEM_WIDTH].reshape(b, m, N_MEM_HEADS, MEM_HEAD_DIM), ck_gain)
    v = kv[..., MEM_WIDTH:].reshape(b, m, N_MEM_HEADS, MEM_HEAD_DIM)
    s = jnp.einsum('bthd,bmhd->bhtm', q, k, preferred_element_type=jnp.float32) * (MEM_HEAD_DIM ** -0.5)
    p = jax.nn.softmax(s, axis=-1).astype(v.dtype)
    o = jnp.einsum('bhtm,bmhd->bthd', p, v).reshape(b, t, MEM_WIDTH)
    return o @ w_co


def swiglu(u, w_gu, w_down):
    gu = u @ w_gu
    return (jax.nn.silu(gu[..., :D_FF]) * gu[..., D_FF:]) @ w_down


def setup_inputs(seed: int = 0) -> dict:
    key = jax.random.key(seed)
    ks = jax.random.split(key, 24)
    L, D = DEPTH, D_MODEL

    def w(k, shape, fan_in):
        return jax.random.normal(k, shape, jnp.float32) * (fan_in ** -0.5)

    def gain(k, shape):
        return 1.0 + 0.05 * jax.random.normal(k, shape, jnp.float32)

    return {
        "x": jax.random.normal(ks[0], (BATCH, SEQ, D), jnp.float32),
        "mem": jax.random.normal(ks[1], (BATCH, MEM_TOKENS, D), jnp.float32),
        "norm_mix": gain(ks[2], (L, D)),
        "w_in": w(ks[3], (L, D, IN_WIDTH), D),
        "b_f": 2.0 + 0.5 * jax.random.normal(ks[4], (L, N_FOX_HEADS), jnp.float32),
        "q_gain": gain(ks[5], (L, FOX_HEAD_DIM)),
        "k_gain": gain(ks[6], (L, FOX_HEAD_DIM)),
        "conv_w": w(ks[7], (L, CONV_KERNEL, CONV_WIDTH), CONV_KERNEL),
        "w_up_a": w(ks[8], (L, FOX_WIDTH, D), FOX_WIDTH),
        "w_up_b": w(ks[9], (L, CONV_WIDTH, D), CONV_WIDTH),
        "w_o": w(ks[10], (L, D, D), 2 * D),
        "norm_mem_q": gain(ks[11], (L, D)),
        "norm_mem_kv": gain(ks[12], (L, D)),
        "w_cq": w(ks[13], (L, D, MEM_WIDTH), D),
        "w_ckv": w(ks[14], (L, D, 2 * MEM_WIDTH), D),
        "cq_gain": gain(ks[15], (L, MEM_HEAD_DIM)),
        "ck_gain": gain(ks[16], (L, MEM_HEAD_DIM)),
        "w_co": w(ks[17], (L, MEM_WIDTH, D), MEM_WIDTH),
        "norm_ffn": gain(ks[18], (L, D)),
        "w_gu": w(ks[19], (L, D, 2 * D_FF), D),
        "w_down": w(ks[20], (L, D_FF, D), D_FF),
    }


def reference(x, mem, norm_mix, w_in, b_f, q_gain, k_gain, conv_w, w_up_a, w_up_b, w_o,
              norm_mem_q, norm_mem_kv, w_cq, w_ckv, cq_gain, ck_gain, w_co,
              norm_ffn, w_gu, w_down):
    h = x
    for l in range(DEPTH):
        h = h + hybrid_mixer(rmsnorm(h, norm_mix[l]), w_in[l], b_f[l], q_gain[l], k_gain[l],
                             conv_w[l], w_up_a[l], w_up_b[l], w_o[l])
        h = h + memory_attention(rmsnorm(h, norm_mem_q[l]), rmsnorm(mem, norm_mem_kv[l]),
                                 w_cq[l], w_ckv[l], cq_gain[l], ck_gain[l], w_co[l])
        h = h + swiglu(rmsnorm(h, norm_ffn[l]), w_gu[l], w_down[l])
    return h
```

```python
import numpy as np
import ml_dtypes
from contextlib import ExitStack
import concourse.bass as bass
import concourse.mybir as mybir
from concourse.bass_utils import run_bass_kernel_spmd

F32 = mybir.dt.float32
BF16 = mybir.dt.bfloat16
AF = mybir.ActivationFunctionType
ALU = mybir.AluOpType

D = 1024
T = 2048
DEPTH = 4
NB = 8
MEMT = 256
DFF = 2816
INW = 5128
EPS = 1e-6
NEG = -30000.0

ENGS = ("pe", "act", "dve", "pool", "sp")


class Res:
    __slots__ = ("w", "r")

    def __init__(self):
        self.w = None
        self.r = {}


class Op:
    __slots__ = ("eng", "fn", "deps", "signal", "dma_sem", "val")

    def __init__(self, eng, fn, dma_sem):
        self.eng = eng
        self.fn = fn
        self.deps = []
        self.signal = False
        self.dma_sem = dma_sem
        self.val = None

    def key(self):
        return self.dma_sem if self.dma_sem is not None else self.eng


class Sched:
    def __init__(self, same_engine_sync=True):
        self.ops = {e: [] for e in ENGS}
        self.same_engine_sync = same_engine_sync
        self.dma_keys = []
        self.last_real = {}
        self.last_dma = {}
        self.dma_eng = {}

    def op(self, eng, fn, reads=(), writes=(), dma_sem=None):
        o = Op(eng, fn, dma_sem)
        if dma_sem is not None:
            if dma_sem not in self.dma_keys:
                self.dma_keys.append(dma_sem)
                self.dma_eng[dma_sem] = eng
            assert self.dma_eng[dma_sem] == eng
        deps = {}
        for r in reads:
            if r.w is not None:
                deps[id(r.w)] = r.w
        for w in writes:
            if w.w is not None:
                deps[id(w.w)] = w.w
            for rd in w.r.values():
                deps[id(rd)] = rd
        for d in deps.values():
            if d.dma_sem is None and o.dma_sem is None and d.eng == o.eng:
                if d.eng == "pe" or not self.same_engine_sync:
                    continue
            if d.dma_sem is not None and d.dma_sem == o.dma_sem:
                continue
            o.deps.append(d)
            d.signal = True
        for r in reads:
            r.r[o.key()] = o
        for w in writes:
            w.w = o
            w.r = {}
        self.ops[eng].append(o)
        if dma_sem is not None:
            self.last_dma[dma_sem] = o
        else:
            self.last_real[eng] = o
        return o

    def barrier(self, engines=("pe", "act", "dve", "sp"), dma_prefixes=("s_",)):
        lasts = [(e, self.last_real[e]) for e in engines if e in self.last_real]
        dmas = [o for k, o in self.last_dma.items() if k.startswith(dma_prefixes)]
        for e in engines:
            o = Op(e, None, None)
            for e2, l in lasts:
                if e2 != e:
                    o.deps.append(l)
                    l.signal = True
            for d in dmas:
                o.deps.append(d)
            self.ops[e].append(o)

    def emit(self, nc, es, final_waits=()):
        sems = {}
        for e in ENGS:
            sems[e] = es.enter_context(nc.semaphore("sem_" + e))
        for k in self.dma_keys:
            sems[k] = es.enter_context(nc.semaphore("dsem_" + k))
        cnt = {}
        for e in ENGS:
            for o in self.ops[e]:
                k = o.key()
                if o.fn is None:
                    continue
                if o.dma_sem is not None:
                    cnt[k] = cnt.get(k, 0) + 16
                    o.val = cnt[k]
                elif o.signal:
                    cnt[k] = cnt.get(k, 0) + 1
                    o.val = cnt[k]
        self.counts = cnt
        block = es.enter_context(nc.Block())
        engobj = {"pe": "tensor", "act": "scalar", "dve": "vector", "pool": "gpsimd", "sp": "sync"}

        def make(e):
            def body(eng):
                waited = {}
                for o in self.ops[e]:
                    for d in o.deps:
                        k = d.key()
                        if waited.get(k, 0) >= d.val:
                            continue
                        eng.wait_ge(sems[k], d.val)
                        waited[k] = d.val
                    if o.fn is None:
                        continue
                    ins = getattr(eng, o.fn[0])(**o.fn[1])
                    if o.dma_sem is not None:
                        ins.then_inc(sems[o.dma_sem], 16)
                    elif o.signal:
                        ins.then_inc(sems[e], 1)
                if e == "sp":
                    for d in final_waits:
                        eng.wait_ge(sems[d.key()], d.val)
            return body

        for e in ENGS:
            getattr(block, engobj[e])(make(e))


class Rot:
    def __init__(self, items):
        self.items = items
        self.i = 0

    def next(self):
        it = self.items[self.i % len(self.items)]
        self.i += 1
        return it


def _host_consts():
    cb = np.zeros((128, 30 * 128), np.float32)
    cb[:, 0:128] = np.eye(128)
    s = np.arange(128)[:, None]
    t = np.arange(128)[None, :]
    cb[:, 128:256] = np.where(s <= t, 0.0, NEG)
    bd = np.zeros((128, 128))
    bd[:64, :64] = 1.0 / 64
    bd[64:, 64:] = 1.0 / 64
    cb[:, 256:384] = bd
    cb[:, 384:512] = 1.0 / 128
    cb[:, 512:640] = 1.0 / 1024
    cb[:, 640:768] = 1.0
    for h in range(8):
        base = 64 if h % 2 == 0 else 0
        sq = np.zeros((128, 128))
        sk = np.zeros((128, 128))
        for j, r in enumerate((h, 32 + h, 64 + h)):
            sq[r, base + j] = -1.0
            sk[r, base + 3 + j] = 1.0
        for j in range(3):
            sq[96, base + 3 + j] = 1.0
            sk[96, base + j] = 1.0
        cb[:, 768 + h * 128: 768 + (h + 1) * 128] = sq
        cb[:, 768 + (8 + h) * 128: 768 + (9 + h) * 128] = sk
        pr = h // 2
        cb[:, 2816 + pr * 128: 2816 + (pr + 1) * 128] += sq
        cb[:, 2816 + (4 + pr) * 128: 2816 + (5 + pr) * 128] += sk
    return cb.astype(ml_dtypes.bfloat16), np.eye(128, dtype=np.float32)


PV_GMIX, PV_GMQ, PV_GMKV, PV_GFFN = 0, 32, 64, 96
PV_QG, PV_KG, PV_CQG, PV_CKG = 128, 132, 136, 140
PV_CONV = 144
PV_BF = 192
PV_N = 196


def _host_pvec(inp):
    pv = np.zeros((128, PV_N), np.float32)
    p = np.arange(128)
    for l in range(DEPTH):
        for k in range(8):
            pv[:, PV_GMIX + l * 8 + k] = inp["norm_mix"][l, k * 128 + p]
            pv[:, PV_GMQ + l * 8 + k] = inp["norm_mem_q"][l, k * 128 + p]
            pv[:, PV_GMKV + l * 8 + k] = inp["norm_mem_kv"][l, k * 128 + p]
            pv[:, PV_GFFN + l * 8 + k] = inp["norm_ffn"][l, k * 128 + p]
        pv[:, PV_QG + l] = inp["q_gain"][l, p % 64]
        pv[:, PV_KG + l] = inp["k_gain"][l, p % 64]
        pv[:, PV_CQG + l] = inp["cq_gain"][l, p]
        pv[:, PV_CKG + l] = inp["ck_gain"][l, p]
        for i in range(3):
            for c in range(4):
                pv[:, PV_CONV + (l * 3 + i) * 4 + c] = inp["conv_w"][l, i, c * 128 + p]
        bf = np.zeros(128, np.float32)
        for base in (0, 32, 64):
            bf[base:base + 8] = inp["b_f"][l]
        pv[:, PV_BF + l] = bf
    return pv


WNAMES = ["w_in", "w_up_a", "w_up_b", "w_o", "w_cq", "w_ckv", "w_co", "w_gu", "w_down"]
WSHAPES = {"w_in": [DEPTH, D, INW], "w_up_a": [DEPTH, 512, D], "w_up_b": [DEPTH, 512, D], "w_o": [DEPTH, D, D],
           "w_cq": [DEPTH, D, 512], "w_ckv": [DEPTH, D, 1024], "w_co": [DEPTH, 512, D],
           "w_gu": [DEPTH, D, 2 * DFF], "w_down": [DEPTH, DFF, D]}


def build(nl=DEPTH, dbg=None, ses=True, stop=None):
    nc = bass.Bass("TRN2", target_bir_lowering=False)
    x_d = nc.dram_tensor("x", [T, D], F32, kind="ExternalInput").ap()
    mem_d = nc.dram_tensor("mem", [MEMT, D], F32, kind="ExternalInput").ap()
    cbf_d = nc.dram_tensor("cbf", [128, 30 * 128], BF16, kind="ExternalInput").ap()
    idf_d = nc.dram_tensor("idf", [128, 128], F32, kind="ExternalInput").ap()
    pv_d = nc.dram_tensor("pvec", [128, PV_N], F32, kind="ExternalInput").ap()
    W = {n: nc.dram_tensor(n, WSHAPES[n], F32, kind="ExternalInput").ap() for n in WNAMES}
    out_d = nc.dram_tensor("out", [T, D], F32, kind="ExternalOutput").ap()
    dbg_d = None
    if dbg is not None:
        dbg_d = nc.dram_tensor("dbg", [128, 8, 2048], F32, kind="ExternalOutput").ap()

    es = ExitStack()
    with es:
        def sb(name, shape, dt):
            return es.enter_context(nc.sbuf_tensor(name, shape, dt))

        hT = sb("hT", [128, 8, T], F32)
        uT = sb("uT", [128, 8, T], BF16)
        RB = sb("RB", [128, 36864], BF16)
        WS = sb("WS", [128, 2, 3072], BF16)
        cbf = sb("cbf_s", [128, 30 * 128], BF16)
        idf = sb("idf_s", [128, 128], F32)
        pv = sb("pv_s", [128, PV_N], F32)
        pv2 = sb("pv2_s", [128, 16], F32)
        wfs = sb("wfs", [128, 8, 8], BF16)
        wfr = sb("wfr", [128, 8, 96], BF16)
        sqp_t = [sb(f"sq{i}", [128, 512], BF16) for i in range(2)]
        sq3_t = [sb(f"sq3_{i}", [128, 512], BF16) for i in range(2)]
        f32_t = [sb(f"f32_{i}", [128, 512], F32) for i in range(4)]
        pt_t = [sb(f"pt{i}", [128, 512], BF16) for i in range(4)]
        sm_t = sb("small", [128, 16], F32)
        ps_t = [es.enter_context(nc.psum_tensor(f"ps{i}", [128, 512], F32)) for i in range(8)]

        S = Sched(same_engine_sync=ses)

        def A(eng, meth, reads, writes, **kw):
            return S.op(eng, (meth, kw), reads=reads, writes=writes)

        def DMA(eng, key, reads, writes, **kw):
            return S.op(eng, ("dma_start", kw), reads=reads, writes=writes, dma_sem=key)

        sqp = Rot([(t, Res()) for t in sqp_t])
        sq3p = Rot([(t, Res()) for t in sq3_t])
        f32p = Rot([(t, Res()) for t in f32_t])
        ptp = Rot([(t, Res()) for t in pt_t])
        r_ps = [Res() for _ in range(8)]
        gen_banks = Rot([(ps_t[i], r_ps[i]) for i in range(6)])
        o_banks = Rot([(ps_t[i], r_ps[i]) for i in (6, 7)])
        all_banks = Rot([(ps_t[i], r_ps[i]) for i in range(8)])
        r_ws = [Res(), Res()]
        r_hT = [[Res() for _ in range(4)] for _ in range(8)]
        r_uT = [[Res() for _ in range(4)] for _ in range(8)]
        r_const = Res()
        r_pv2 = Res()
        r_wfs, r_wfr = Res(), Res()
        r_small = Res()

        IDENT = cbf[:, 0:128]
        MASKB = cbf[:, 128:256]
        BD64 = cbf[:, 256:384]
        ONES128 = cbf[:, 384:512]
        ONESD = cbf[:, 512:640]
        ONE1 = cbf[:, 640:768]

        def SELQ(h):
            return cbf[:, 768 + h * 128: 768 + (h + 1) * 128]

        def SELK(h):
            return cbf[:, 768 + (8 + h) * 128: 768 + (9 + h) * 128]

        def pcol(c):
            return pv[:, c:c + 1]

        def tbs(tb):
            return slice(tb * 512, (tb + 1) * 512)

        def rbv(off, n):
            return RB[:, off:off + n]

        VAUG = [rbv(s * 3072, 3072).rearrange("p (i c) -> p i c", c=192) for s in range(2)]
        QK = [rbv(6144 + i * 2048, 2048) for i in range(4)]
        QKB = [QK, [rbv(28672 + i * 2048, 2048) for i in range(4)]]
        CF = rbv(14336, 4096).bitcast(F32)
        CTAB = rbv(18432, 2048)
        AOUT = rbv(20480, 8192).rearrange("p (c t) -> p c t", c=4)
        COUT = rbv(28672, 8192).rearrange("p (c t) -> p c t", c=4)
        XB = rbv(0, 4104).bitcast(F32)
        MERGED = rbv(0, 16384).rearrange("p (c t) -> p c t", c=8)
        QMT = rbv(0, 8192).rearrange("p (c t) -> p c t", c=4)
        OMT = rbv(8192, 8192).rearrange("p (c t) -> p c t", c=4)
        MSTG = rbv(16384, 4096).bitcast(F32).rearrange("p (i d) -> p i d", i=2)
        MJUNK = rbv(20480, 1024)
        MEMNT = rbv(28672, 2048).rearrange("p (k t) -> p k t", k=8)
        KMT = rbv(30720, 1024).rearrange("p (h t) -> p h t", h=4)
        VM = rbv(31744, 1024).rearrange("p (i c) -> p i c", i=2)
        ACTT = rbv(0, 16384).rearrange("p (c t) -> p c t", c=8)
        GUW = [rbv(16384 + s * 4096, 4096).rearrange("p (g k n) -> p g k n", g=2, k=8) for s in range(3)]
        XS = [rbv(i * 2048, 2048).bitcast(F32) for i in range(2)]

        r_vaug = [[Res() for _ in range(4)] for _ in range(2)]
        r_qk = [[[Res(), Res()] for _ in range(4)] for _ in range(4)]
        r_qkb = [r_qk, [[[Res(), Res()] for _ in range(4)] for _ in range(4)]]
        r_cf = [Res() for _ in range(4)]
        r_ctab = Res()
        r_aout = [[[Res(), Res()] for _ in range(4)] for _ in range(4)]
        r_cout = [[Res() for _ in range(4)] for _ in range(4)]
        r_xb = [Res() for _ in range(5)]
        r_merged = [[Res() for _ in range(4)] for _ in range(8)]
        r_qmt = [[Res() for _ in range(4)] for _ in range(4)]
        r_omt = [[Res() for _ in range(4)] for _ in range(4)]
        r_mstg, r_mjunk, r_memnt = Res(), Res(), Res()
        r_kmt = [Res() for _ in range(4)]
        r_vm = Res()
        r_actt = [[Res() for _ in range(4)] for _ in range(8)]
        r_guw = [Res() for _ in range(3)]
        r_xs = [Res(), Res()]

        pool_dmas = []

        def wdma(dst, src, res, key):
            o = DMA("pool", key, [], [res], out=dst, in_=src, max_dma_last_dim=2048)
            pool_dmas.append(o)
            return o

        def wcols(name, l, c0, n):
            return W[name][l].rearrange("(k p) n -> p k n", p=128)[:, :, c0:c0 + n]

        def wsv(s, off, kc, n):
            return WS[:, s, off:off + kc * n].rearrange("p (k n) -> p k n", k=kc)

        def mm(ps, lhsT, rhs, start, stop, reads, rps):
            A("pe", "matmul", reads, [rps], out=ps, lhsT=lhsT, rhs=rhs, start=start, stop=stop)

        def proj(ps, rps, wv, act_fn, nk, wres, ares_fn):
            for k in range(nk):
                mm(ps, wv[:, k, :], act_fn(k), k == 0, k == nk - 1, [wres] + list(ares_fn(k)), rps)

        def dump(ap, res, slot, is_bf16=False, np_=128):
            if dbg_d is None:
                return None
            n = ap.shape[-1]
            dst = dbg_d[0:np_, slot, 0:n]
            if is_bf16:
                return DMA("pool", "s_dbgp", list(res), [], out=dst, in_=ap, max_dma_last_dim=1024)
            return DMA("sp", "s_dbg", list(res), [], out=dst, in_=ap)

        def rmsnorm_T(gbase, stat=None, after_tb=None):
            for tb in range(4):
                msb, rms = stat if stat is not None else gen_banks.next()
                for k in range(8):
                    sq, rsq = sqp.next()
                    A("act", "activation", [r_hT[k][tb]], [rsq], out=sq[:, :], in_=hT[:, k, tbs(tb)], func=AF.Square)
                    mm(msb[:, :], ONESD, sq[:, :], k == 0, k == 7, [rsq, r_const], rms)
                rstd, rr = f32p.next()
                A("act", "activation", [rms, r_pv2], [rr], out=rstd[:, :], in_=msb[:, :], func=AF.Ln, bias=pv2[:, 12:13], scale=1.0)
                A("act", "activation", [rr], [rr], out=rstd[:, :], in_=rstd[:, :], func=AF.Exp, scale=-0.5)
                for k in range(8):
                    A("dve", "scalar_tensor_tensor", [r_hT[k][tb], rr, r_const], [r_uT[k][tb]],
                      out=uT[:, k, tbs(tb)], in0=hT[:, k, tbs(tb)], scalar=pcol(gbase + k), in1=rstd[:, :],
                      op0=ALU.mult, op1=ALU.mult)
                if after_tb is not None:
                    after_tb(tb)

        def resid_add(ps, rps, m, tb):
            A("dve", "tensor_tensor", [rps, r_hT[m][tb]], [r_hT[m][tb]],
              out=hT[:, m, tbs(tb)], in0=ps[:, :], in1=hT[:, m, tbs(tb)], op=ALU.add)

        def qknorm(pb, rpb, N, statmat, banks):
            sq, rsq = sqp.next()
            A("act", "activation", [rpb], [rsq], out=sq[:, 0:N], in_=pb[:, 0:N], func=AF.Square)
            msb, rms = banks.next()
            mm(msb[:, 0:N], statmat, sq[:, 0:N], True, True, [rsq, r_const], rms)
            rstd, rr = f32p.next()
            A("act", "activation", [rms, r_pv2], [rr], out=rstd[:, 0:N], in_=msb[:, 0:N], func=AF.Ln, bias=pv2[:, 12:13], scale=1.0)
            A("act", "activation", [rr], [rr], out=rstd[:, 0:N], in_=rstd[:, 0:N], func=AF.Exp, scale=-0.5)
            return rstd, rr

        SK = set()
        if "cbf" not in SK:
            DMA("sp", "s_c", [], [r_const], out=cbf[:, :], in_=cbf_d[:, :])
        DMA("sp", "s_c", [], [r_const], out=idf[:, :], in_=idf_d[:, :])
        if "pv" not in SK:
            DMA("sp", "s_c", [], [r_const], out=pv[:, :], in_=pv_d[:, :])
        if "pv2" not in SK:
          A("dve", "tensor_scalar", [r_const], [r_pv2], out=pv2[:, 0:4], in0=pv[:, PV_QG:PV_QG + 4], scalar1=0.125, scalar2=None, op0=ALU.mult)
        if "pv2" not in SK:
          A("dve", "tensor_scalar", [r_const], [r_pv2], out=pv2[:, 4:8], in0=pv[:, PV_CQG:PV_CQG + 4], scalar1=float(128 ** -0.5), scalar2=None, op0=ALU.mult)
        if "pv2" not in SK:
          A("dve", "tensor_scalar", [r_const], [r_pv2], out=pv2[:, 8:12], in0=pv[:, PV_BF:PV_BF + 4], scalar1=-1.0, scalar2=None, op0=ALU.mult)
        if "ms" not in SK:
            A("dve", "memset", [], [r_wfr], ap=wfr[:, :, :], constant=0.0)
        if "ms2" not in SK:
            A("dve", "memset", [r_pv2], [r_pv2], ap=pv2[:, 12:13], constant=EPS)
            A("dve", "memset", [r_pv2], [r_pv2], ap=pv2[:, 13:14], constant=1.0)

        for i in range(16):
            xs, rxs = XS[i % 2], r_xs[i % 2]
            DMA("sp", f"s_x{i % 2}", [], [rxs], out=xs[:, :], in_=x_d[i * 128:(i + 1) * 128, :])
            for g in range(2):
                pb, rpb = all_banks.next()
                for kk in range(4):
                    k = g * 4 + kk
                    A("pe", "transpose", [rxs, r_const], [rpb], out=pb[:, kk * 128:(kk + 1) * 128], in_=xs[:, k * 128:(k + 1) * 128], identity=idf[:, :])
                tsl = slice(i * 128, (i + 1) * 128)
                wr = [r_hT[k_][i // 4] for k_ in range(g * 4, g * 4 + 4)]
                src = pb[:, :].rearrange("p (a b) -> p a b", b=128)
                if g == 0:
                    A("dve", "tensor_copy", [rpb], wr, out=hT[:, g * 4:(g + 1) * 4, tsl], in_=src)
                else:
                    A("act", "activation", [rpb], wr, out=hT[:, g * 4:(g + 1) * 4, tsl], in_=src, func=AF.Copy)
        S.barrier()

        def layer(l):
            A("dve", "memset", [], [r_ctab], ap=CTAB[64:128, :], constant=1.0)
            for s in range(2):
                A("dve", "memset", [], r_vaug[s], ap=VAUG[s][:, :, 64:128], constant=1.0)
            wdma(wfs[:, :, :], wcols("w_in", l, 1536, 8), r_wfs, "wf")
            for base in (0, 32, 64):
                A("act", "activation", [r_wfs], [r_wfr], out=wfr[:, :, base:base + 8], in_=wfs[:, :, :], func=AF.Copy)

            def f_path(tb):
                fb, rfb = ps_t[1], r_ps[1]
                proj(fb[0:96, :], rfb, wfr, lambda k: uT[:, k, tbs(tb)], 8, r_wfr, lambda k: [r_uT[k][tb]])
                t, rt = f32p.next()
                A("act", "activation", [rfb, r_pv2], [rt], out=t[0:96, :], in_=fb[0:96, :], func=AF.Exp, bias=pv2[0:96, 8 + l:9 + l], scale=-1.0)
                A("act", "activation", [rt, r_pv2], [rt], out=t[0:96, :], in_=t[0:96, :], func=AF.Ln, bias=pv2[0:96, 13:14], scale=1.0)
                init = 0.0 if tb == 0 else CF[0:96, tb * 512 - 1:tb * 512]
                A("dve", "tensor_tensor_scan", [rt] + ([r_cf[tb - 1]] if tb else []), [r_cf[tb]],
                  out=CF[0:96, tbs(tb)], data0=t[0:96, :], data1=t[0:96, :], initial=init, op0=ALU.add, op1=ALU.max)

            def decay_split():
                A("dve", "tensor_copy", r_cf, [r_ctab], out=CTAB[0:96, :], in_=CF[0:96, :])
                A("dve", "tensor_tensor", [r_ctab] + r_cf, r_cf, out=CF[0:96, :], in0=CF[0:96, :], in1=CTAB[0:96, :], op=ALU.subtract)
                A("dve", "tensor_copy", r_cf, [r_ctab], out=CTAB[0:64, :], in_=CF[0:64, :])
                A("dve", "tensor_tensor", [r_ctab] + r_cf, r_cf, out=CF[0:64, :], in0=CF[0:64, :], in1=CTAB[0:64, :], op=ALU.subtract)
                A("dve", "tensor_copy", r_cf, [r_ctab], out=CTAB[0:32, :], in_=CF[0:32, :])

            SB_ = [(ps_t[i], r_ps[i]) for i in (0, 1, 2)]
            OBK = [(ps_t[i], r_ps[i]) for i in (3, 4)]
            PJB = Rot([(ps_t[i], r_ps[i]) for i in (5, 6, 7)])
            s_rot = Rot(SB_)
            o_rot = Rot(OBK)

            def qkv_units(pr):
                s = pr % 2
                qb = pr % 2
                QKb, rqkb = QKB[qb], r_qkb[qb]
                hA, hB = 2 * pr, 2 * pr + 1
                wq, wk, wv = wsv(s, 0, 8, 128), wsv(s, 1024, 8, 128), wsv(s, 2048, 8, 128)
                wdma(wq, wcols("w_in", l, pr * 128, 128), r_ws[s], f"w{s}")
                wdma(wk, wcols("w_in", l, 512 + pr * 128, 128), r_ws[s], f"w{s}")
                wdma(wv, wcols("w_in", l, 1024 + pr * 128, 128), r_ws[s], f"w{s}")
                units = []
                struct = {"ab": [], "aug": [], "v": []}
                for tb in range(4):
                    st = {}

                    def uA(which, wmat, tb=tb, st=st):
                        pb, rpb = PJB.next()
                        proj(pb[:, :], rpb, wmat, lambda k: uT[:, k, tbs(tb)], 8, r_ws[s], lambda k: [r_uT[k][tb]])
                        sq, rsq = sqp.next()
                        A("act", "activation", [rpb], [rsq], out=sq[:, :], in_=pb[:, :], func=AF.Square)
                        st[which] = (pb, rpb, sq, rsq)

                    def uB(which, gcol, ia, ib, tb=tb, st=st):
                        pb, rpb, sq, rsq = st[which]
                        msb, rms = PJB.next()
                        mm(msb[:, :], BD64, sq[:, :], True, True, [rsq, r_const], rms)
                        rstd, rr = f32p.next()
                        A("act", "activation", [rms, r_pv2], [rr], out=rstd[:, :], in_=msb[:, :], func=AF.Ln, bias=pv2[:, 12:13], scale=1.0)
                        A("act", "activation", [rr], [rr], out=rstd[:, :], in_=rstd[:, :], func=AF.Exp, scale=-0.5)
                        A("dve", "scalar_tensor_tensor", [rpb, rr, r_const, r_pv2], [rqkb[ia][tb][0]],
                          out=QKb[ia][0:64, tbs(tb)], in0=pb[0:64, :], scalar=gcol[0:64, :], in1=rstd[0:64, :], op0=ALU.mult, op1=ALU.mult)
                        A("dve", "scalar_tensor_tensor", [rpb, rr, r_const, r_pv2], [rqkb[ib][tb][0]],
                          out=QKb[ib][64:128, tbs(tb)], in0=pb[64:128, :], scalar=gcol[64:128, :], in1=rstd[64:128, :], op0=ALU.mult, op1=ALU.mult)

                    def uAug(sel, ita, itb, tb=tb):
                        ab, rab = PJB.next()
                        mm(ab[:, :], sel, CTAB[:, tbs(tb)], True, True, [r_const, r_ctab], rab)
                        A("dve", "tensor_copy", [rab], [rqkb[ita][tb][1]], out=QKb[ita][64:128, tbs(tb)], in_=ab[64:128, :])
                        A("dve", "tensor_copy", [rab], [rqkb[itb][tb][1]], out=QKb[itb][0:64, tbs(tb)], in_=ab[0:64, :])

                    selq = cbf[:, 2816 + pr * 128: 2816 + (pr + 1) * 128]
                    selk = cbf[:, 2816 + (4 + pr) * 128: 2816 + (5 + pr) * 128]
                    ua_q = lambda uA=uA: uA("q", wq)
                    ua_k = lambda uA=uA: uA("k", wk)
                    ub_q = lambda uB=uB: uB("q", pv2[:, l:l + 1], 0, 1)
                    ub_k = lambda uB=uB: uB("k", pcol(PV_KG + l), 2, 3)
                    ug_q = lambda uAug=uAug: uAug(selq, 0, 1)
                    ug_k = lambda uAug=uAug: uAug(selk, 2, 3)
                    units += [ua_q, ub_q, ug_q, ua_k, ub_k, ug_k]
                    struct["ab"].append([ua_q, ub_q, ua_k, ub_k])
                    struct["aug"] += [ug_q, ug_k]
                for g in range(4):
                    def uV(g=g):
                        vb, rvb = PJB.next()
                        for ii in range(4):
                            i = 4 * g + ii
                            for k in range(8):
                                mm(vb[:, ii * 128:(ii + 1) * 128], uT[:, k, i * 128:(i + 1) * 128], wv[:, k, :], k == 0, k == 7,
                                   [r_ws[s], r_uT[k][g]], rvb)
                        vb3 = vb[:, :].rearrange("p (a b) -> p a b", b=128)
                        A("dve", "tensor_copy", [rvb], [r_vaug[s][g]], out=VAUG[s][:, 4 * g:4 * g + 4, 0:64], in_=vb3[:, :, 0:64])
                        A("dve", "tensor_copy", [rvb], [r_vaug[s][g]], out=VAUG[s][:, 4 * g:4 * g + 4, 128:192], in_=vb3[:, :, 64:128])
                    units.append(uV)
                    struct["v"].append(uV)
                return units, struct

            def attn_steps(pr):
                s = pr % 2
                qb = pr % 2
                QKb, rqkb = QKB[qb], r_qkb[qb]
                for hh in range(2):
                    Qt, Kt = QKb[hh], QKb[2 + hh]
                    rq, rk = rqkb[hh], rqkb[2 + hh]
                    vlo = 0 if hh == 0 else 64
                    olo, dlo = (0, 64) if hh == 0 else (64, 0)
                    tiles = [(j, i) for j in range(4) for i in range(4 * j + 4)]
                    pend = []
                    obank = {}

                    def emit_S(j, i):
                        r = i - 4 * j
                        c0 = 128 * max(r, 0)
                        N = 512 - c0
                        sbk, rsb = s_rot.next()
                        q0 = j * 512 + c0
                        qreads = [rq[j][0], rq[j][1], rk[i // 4][0], rk[i // 4][1]]
                        mm(sbk[:, 0:N], Kt[:, i * 128:(i + 1) * 128], Qt[:, q0:(j + 1) * 512], True, r < 0, qreads, rsb)
                        if r >= 0:
                            mm(sbk[:, 0:128], IDENT, MASKB, False, True, [r_const], rsb)
                        pt, rpt = ptp.next()
                        A("act", "activation", [rsb], [rpt], out=pt[:, 0:N], in_=sbk[:, 0:N], func=AF.Exp)
                        return (j, i, c0, N, pt, rpt)

                    def emit_PV(j, i, c0, N, pt, rpt):
                        n = 4 * j + 4
                        if i == 0:
                            obank[j] = o_rot.next()
                        ob, rob = obank[j]
                        mm(ob[:, c0:512], VAUG[s][:, i, vlo:vlo + 128], pt[:, 0:N], i == 0, i == n - 1, [rpt, r_vaug[s][i // 4]], rob)
                        if i == n - 1:
                            def fin(ob=ob, rob=rob, j=j):
                                rec, rrec = f32p.next()
                                A("act", "activation", [rob], [rrec], out=rec[dlo:dlo + 64, :], in_=ob[dlo:dlo + 64, :], func=AF.Ln)
                                A("act", "activation", [rrec], [rrec], out=rec[dlo:dlo + 64, :], in_=rec[dlo:dlo + 64, :], func=AF.Exp, scale=-1.0)
                                A("dve", "tensor_tensor", [rob, rrec], [r_aout[pr][j][hh]],
                                  out=AOUT[olo:olo + 64, pr, tbs(j)], in0=ob[olo:olo + 64, :], in1=rec[dlo:dlo + 64, :], op=ALU.mult)
                            deferred.append([3, fin])

                    LA = 2
                    deferred = []

                    def tick():
                        for d in list(deferred):
                            d[0] -= 1
                            if d[0] <= 0:
                                deferred.remove(d)
                                d[1]()

                    for (j, i) in tiles:
                        pend.append(emit_S(j, i))
                        if len(pend) > LA:
                            emit_PV(*pend.pop(0))
                        tick()
                        yield
                    while pend:
                        emit_PV(*pend.pop(0))
                    while deferred:
                        tick()

            _, st0 = qkv_units(0)

            def after1(tb):
                f_path(tb)
                for u in st0["ab"][tb]:
                    u()
                st0["v"][tb]()

            rmsnorm_T(PV_GMIX + l * 8, stat=(ps_t[0], r_ps[0]), after_tb=after1)
            decay_split()
            for u in st0["aug"]:
                u()
            for pr in range(4):
                nxt = qkv_units(pr + 1)[0] if pr < 3 else []
                cnt = 0
                for _ in attn_steps(pr):
                    cnt += 1
                    if cnt % 2 == 0 and nxt:
                        nxt.pop(0)()
                while nxt:
                    nxt.pop(0)()
            if dbg == "aout" and l == 0:
                for c in range(4):
                    dump(AOUT[:, c, :], [r_aout[c][j_][hf] for j_ in range(4) for hf in range(2)], c, is_bf16=True)
            S.barrier()

            if stop == "s5":
                return
            A("dve", "memset", [], [r_xb[0]], ap=XB[:, 0:2], constant=0.0)
            for c in range(4):
                s = c % 2
                wz, wgb, wgc = wsv(s, 0, 8, 128), wsv(s, 1024, 8, 128), wsv(s, 2048, 8, 128)
                wdma(wz, wcols("w_in", l, 1544 + c * 128, 128), r_ws[s], f"w{s}")
                wdma(wgb, wcols("w_in", l, 2056 + c * 128, 128), r_ws[s], f"w{s}")
                wdma(wgc, wcols("w_in", l, 2568 + c * 128, 128), r_ws[s], f"w{s}")
                w0, w1, w2 = (pcol(PV_CONV + (l * 3 + i) * 4 + c) for i in range(3))
                for tb in range(4):
                    ur = lambda k: [r_uT[k][tb]]
                    ua = lambda k: uT[:, k, tbs(tb)]
                    zb, rzb = gen_banks.next()
                    proj(zb[:, :], rzb, wz, ua, 8, r_ws[s], ur)
                    gcb, rgcb = gen_banks.next()
                    proj(gcb[:, :], rgcb, wgc, ua, 8, r_ws[s], ur)
                    gbb, rgbb = gen_banks.next()
                    proj(gbb[:, :], rgbb, wgb, ua, 8, r_ws[s], ur)
                    gcs, rgcs = f32p.next()
                    A("act", "activation", [rgcb], [rgcs], out=gcs[:, :], in_=gcb[:, :], func=AF.Copy)
                    A("dve", "tensor_tensor", [rzb, rgcs], [r_xb[tb + 1]], out=XB[:, 2 + tb * 512:2 + (tb + 1) * 512], in0=zb[:, :], in1=gcs[:, :], op=ALU.mult)
                    y, ry = f32p.next()
                    xr = [r_xb[tb], r_xb[tb + 1], r_const]
                    A("dve", "tensor_scalar", xr, [ry], out=y[:, :], in0=XB[:, tb * 512:tb * 512 + 512], scalar1=w0, scalar2=None, op0=ALU.mult)
                    A("dve", "scalar_tensor_tensor", xr + [ry], [ry], out=y[:, :], in0=XB[:, tb * 512 + 1:tb * 512 + 513], scalar=w1, in1=y[:, :], op0=ALU.mult, op1=ALU.add)
                    A("dve", "scalar_tensor_tensor", xr + [ry], [ry], out=y[:, :], in0=XB[:, tb * 512 + 2:tb * 512 + 514], scalar=w2, in1=y[:, :], op0=ALU.mult, op1=ALU.add)
                    A("dve", "tensor_tensor", [rgbb, ry], [r_cout[c][tb]], out=COUT[:, c, tbs(tb)], in0=gbb[:, :], in1=y[:, :], op=ALU.mult)
            if dbg == "cout" and l == 0:
                for c in range(4):
                    dump(COUT[:, c, :], r_cout[c], c, is_bf16=True)
            S.barrier()

            if stop == "s6":
                return
            for m in range(8):
                s = m % 2
                wga, wgb2, wua, wub = wsv(s, 0, 8, 128), wsv(s, 1024, 8, 128), wsv(s, 2048, 4, 128), wsv(s, 2560, 4, 128)
                wdma(wga, wcols("w_in", l, 3080 + m * 128, 128), r_ws[s], f"w{s}")
                wdma(wgb2, wcols("w_in", l, 4104 + m * 128, 128), r_ws[s], f"w{s}")
                wdma(wua, wcols("w_up_a", l, m * 128, 128), r_ws[s], f"w{s}")
                wdma(wub, wcols("w_up_b", l, m * 128, 128), r_ws[s], f"w{s}")
                for tb in range(4):
                    ur = lambda k: [r_uT[k][tb]]
                    ua = lambda k: uT[:, k, tbs(tb)]
                    gab, rgab = all_banks.next()
                    proj(gab[:, :], rgab, wga, ua, 8, r_ws[s], ur)
                    uab, ruab = all_banks.next()
                    proj(uab[:, :], ruab, wua, lambda k: AOUT[:, k, tbs(tb)], 4, r_ws[s], lambda k: r_aout[k][tb])
                    gbb, rgbb = all_banks.next()
                    proj(gbb[:, :], rgbb, wgb2, ua, 8, r_ws[s], ur)
                    ubb, rubb = all_banks.next()
                    proj(ubb[:, :], rubb, wub, lambda k: COUT[:, k, tbs(tb)], 4, r_ws[s], lambda k: [r_cout[k][tb]])
                    sa, rsa = f32p.next()
                    sb_, rsb_ = f32p.next()
                    A("act", "activation", [rgab], [rsa], out=sa[:, :], in_=gab[:, :], func=AF.Sigmoid)
                    A("act", "activation", [rgbb], [rsb_], out=sb_[:, :], in_=gbb[:, :], func=AF.Sigmoid)
                    A("dve", "tensor_tensor", [ruab, rsa], [rsa], out=sa[:, :], in0=uab[:, :], in1=sa[:, :], op=ALU.mult)
                    A("dve", "tensor_tensor", [rubb, rsb_], [rsb_], out=sb_[:, :], in0=ubb[:, :], in1=sb_[:, :], op=ALU.mult)
                    A("dve", "tensor_tensor", [rsa, rsb_], [r_merged[m][tb]], out=MERGED[:, m, tbs(tb)], in0=sa[:, :], in1=sb_[:, :], op=ALU.add)
            if dbg == "merged" and l == 0:
                for m in range(8):
                    dump(MERGED[:, m, :], r_merged[m], m, is_bf16=True)
            for mp in range(4):
                s = mp % 2
                wo = wsv(s, 0, 8, 256)
                wdma(wo, wcols("w_o", l, mp * 256, 256), r_ws[s], f"w{s}")
                for mi in range(2):
                    m = mp * 2 + mi
                    for tb in range(4):
                        db, rdb = all_banks.next()
                        for k in range(8):
                            mm(db[:, :], wo[:, k, mi * 128:(mi + 1) * 128], MERGED[:, k, tbs(tb)], k == 0, k == 7, [r_ws[s], r_merged[k][tb]], rdb)
                        resid_add(db, rdb, m, tb)
            if dbg == "h1" and l == 0:
                for k in range(8):
                    dump(hT[:, k, :], r_hT[k], k)
            S.barrier()

            if stop == "s7":
                return
            DMA("sp", "s_m", [], [r_mstg], out=MSTG[:, :, :], in_=mem_d.rearrange("(i p) d -> p i d", p=128))
            for i in range(2):
                A("act", "activation", [r_mstg], [r_mjunk, r_small], out=MJUNK[:, :], in_=MSTG[:, i, :], func=AF.Square, accum_out=sm_t[:, i:i + 1])
            A("dve", "tensor_scalar", [r_small], [r_small], out=sm_t[:, 2:4], in0=sm_t[:, 0:2], scalar1=1.0 / D, scalar2=EPS, op0=ALU.mult, op1=ALU.add)
            A("act", "activation", [r_small], [r_small], out=sm_t[:, 4:6], in_=sm_t[:, 2:4], func=AF.Ln)
            A("act", "activation", [r_small], [r_small], out=sm_t[:, 4:6], in_=sm_t[:, 4:6], func=AF.Exp, scale=-0.5)
            for i in range(2):
                A("dve", "tensor_scalar", [r_small, r_mstg], [r_mstg], out=MSTG[:, i, :], in0=MSTG[:, i, :], scalar1=sm_t[:, 4 + i:5 + i], scalar2=None, op0=ALU.mult)
            for i in range(2):
                for g in range(2):
                    pb, rpb = all_banks.next()
                    for kk in range(4):
                        k = g * 4 + kk
                        A("pe", "transpose", [r_mstg, r_const], [rpb], out=pb[:, kk * 128:(kk + 1) * 128], in_=MSTG[:, i, k * 128:(k + 1) * 128], identity=idf[:, :])
                    for kk in range(4):
                        k = g * 4 + kk
                        A("dve", "tensor_scalar", [rpb, r_const], [r_memnt], out=MEMNT[:, k, i * 128:(i + 1) * 128], in0=pb[:, kk * 128:(kk + 1) * 128],
                          scalar1=pcol(PV_GMKV + l * 8 + k), scalar2=None, op0=ALU.mult)
            for hp in range(2):
                s = hp % 2
                wkk = wsv(s, 0, 8, 256)
                wdma(wkk, wcols("w_ckv", l, hp * 256, 256), r_ws[s], f"w{s}")
                for hi_ in range(2):
                    h = hp * 2 + hi_
                    pb, rpb = all_banks.next()
                    for k in range(8):
                        mm(pb[:, 0:256], wkk[:, k, hi_ * 128:(hi_ + 1) * 128], MEMNT[:, k, :], k == 0, k == 7, [r_ws[s], r_memnt], rpb)
                    rstd, rr = qknorm(pb, rpb, 256, ONES128, all_banks)
                    A("dve", "scalar_tensor_tensor", [rpb, rr, r_const], [r_kmt[h]], out=KMT[:, h, :], in0=pb[:, 0:256], scalar=pcol(PV_CKG + l), in1=rstd[:, 0:256],
                      op0=ALU.mult, op1=ALU.mult)
            for vh in range(2):
                s = vh % 2
                wvv = wsv(s, 0, 8, 256)
                wdma(wvv, wcols("w_ckv", l, 512 + vh * 256, 256), r_ws[s], f"w{s}")
                for i in range(2):
                    pb, rpb = all_banks.next()
                    for k in range(8):
                        mm(pb[:, 0:256], MEMNT[:, k, i * 128:(i + 1) * 128], wvv[:, k, :], k == 0, k == 7, [r_ws[s], r_memnt], rpb)
                    A("act", "activation", [rpb], [r_vm], out=VM[:, i, vh * 256:(vh + 1) * 256], in_=pb[:, 0:256], func=AF.Copy)
            wqs = []
            for hp in range(2):
                wqq = wsv(hp, 0, 8, 256)
                wdma(wqq, wcols("w_cq", l, hp * 256, 256), r_ws[hp], f"w{hp}")
                wqs.append(wqq)
            PJ = [(ps_t[0], r_ps[0]), (ps_t[1], r_ps[1])]
            ST = (ps_t[2], r_ps[2])
            SC = [(ps_t[3], r_ps[3]), (ps_t[4], r_ps[4])]
            OB = [(ps_t[5], r_ps[5]), (ps_t[7], r_ps[7])]
            DN = (ps_t[6], r_ps[6])
            st3 = {}

            def p3_s1(n):
                h, tb = n // 4, n % 4
                pb, rpb = PJ[n % 2]
                wqq = wqs[h // 2]
                for k in range(8):
                    mm(pb[:, :], wqq[:, k, (h % 2) * 128:(h % 2 + 1) * 128], uT[:, k, tbs(tb)], k == 0, k == 7, [r_ws[h // 2], r_uT[k][tb]], rpb)
                sq, rsq = sq3p.next()
                A("act", "activation", [rpb], [rsq], out=sq[:, :], in_=pb[:, :], func=AF.Square)
                st3[n] = dict(pb=pb, rpb=rpb, sq=sq, rsq=rsq)

            def p3_s2(n):
                h, tb = n // 4, n % 4
                d = st3[n]
                msb, rms = ST
                mm(msb[:, :], ONES128, d["sq"][:, :], True, True, [d["rsq"], r_const], rms)
                rstd, rr = f32p.next()
                A("act", "activation", [rms, r_pv2], [rr], out=rstd[:, :], in_=msb[:, :], func=AF.Ln, bias=pv2[:, 12:13], scale=1.0)
                A("act", "activation", [rr], [rr], out=rstd[:, :], in_=rstd[:, :], func=AF.Exp, scale=-0.5)
                A("dve", "scalar_tensor_tensor", [d["rpb"], rr, r_pv2], [r_qmt[h][tb]], out=QMT[:, h, tbs(tb)], in0=d["pb"][:, :], scalar=pv2[:, 4 + l:5 + l], in1=rstd[:, :],
                  op0=ALU.mult, op1=ALU.mult)

            def p3_s3(n):
                h, tb = n // 4, n % 4
                pts = []
                for i in range(2):
                    sbk, rsb = SC[i]
                    mm(sbk[:, :], KMT[:, h, i * 128:(i + 1) * 128], QMT[:, h, tbs(tb)], True, True, [r_kmt[h], r_qmt[h][tb]], rsb)
                    pt, rpt = ptp.next()
                    A("act", "activation", [rsb], [rpt], out=pt[:, :], in_=sbk[:, :], func=AF.Exp)
                    pts.append((pt, rpt))
                st3[n]["pts"] = pts

            def p3_s4(n):
                h, tb = n // 4, n % 4
                pts = st3[n]["pts"]
                ob, rob = OB[n % 2]
                dbk, rdbk = DN
                for i, (pt, rpt) in enumerate(pts):
                    mm(dbk[:, :], ONE1, pt[:, :], i == 0, i == 1, [r_const, rpt], rdbk)
                for i, (pt, rpt) in enumerate(pts):
                    mm(ob[:, :], VM[:, i, h * 128:(h + 1) * 128], pt[:, :], i == 0, i == 1, [r_vm, rpt], rob)
                rec, rrec = f32p.next()
                A("act", "activation", [rdbk], [rrec], out=rec[:, :], in_=dbk[:, :], func=AF.Ln)
                A("act", "activation", [rrec], [rrec], out=rec[:, :], in_=rec[:, :], func=AF.Exp, scale=-1.0)
                A("dve", "tensor_tensor", [rob, rrec], [r_omt[h][tb]], out=OMT[:, h, tbs(tb)], in0=ob[:, :], in1=rec[:, :], op=ALU.mult)
                del st3[n]

            def p3_iter(it):
                if it < 16:
                    p3_s1(it)
                if 0 <= it - 1 < 16:
                    p3_s2(it - 1)
                if 0 <= it - 2 < 16:
                    p3_s3(it - 2)
                if 0 <= it - 3 < 16:
                    p3_s4(it - 3)

            rmsnorm_T(PV_GMQ + l * 8, stat=ST, after_tb=p3_iter)
            for it in range(4, 16 + 3):
                p3_iter(it)
            if dbg == "omt" and l == 0:
                for h in range(4):
                    dump(OMT[:, h, :], r_omt[h], h, is_bf16=True)
            for mp in range(4):
                s = mp % 2
                wco = wsv(s, 0, 4, 256)
                wdma(wco, wcols("w_co", l, mp * 256, 256), r_ws[s], f"w{s}")
                for mi in range(2):
                    m = mp * 2 + mi
                    for tb in range(4):
                        db, rdb = all_banks.next()
                        for k in range(4):
                            mm(db[:, :], wco[:, k, mi * 128:(mi + 1) * 128], OMT[:, k, tbs(tb)], k == 0, k == 3, [r_ws[s], r_omt[k][tb]], rdb)
                        resid_add(db, rdb, m, tb)
            if dbg == "h2" and l == 0:
                for k in range(8):
                    dump(hT[:, k, :], r_hT[k], k)
            S.barrier()

            if stop == "s8":
                return
            parts = [(0, 8), (8, 8), (16, 6)]
            gi = 0

            def ffn_tile(gs, ci, cl, tb):
                gb_, rgb_ = all_banks.next()
                for k in range(8):
                    mm(gb_[:, :], GUW[gs][:, 0, k, ci * 128:(ci + 1) * 128], uT[:, k, tbs(tb)], k == 0, k == 7, [r_guw[gs], r_uT[k][tb]], rgb_)
                ub_, rub_ = all_banks.next()
                for k in range(8):
                    mm(ub_[:, :], GUW[gs][:, 1, k, ci * 128:(ci + 1) * 128], uT[:, k, tbs(tb)], k == 0, k == 7, [r_guw[gs], r_uT[k][tb]], rub_)
                sg, rsg = f32p.next()
                A("act", "activation", [rgb_], [rsg], out=sg[:, :], in_=gb_[:, :], func=AF.Silu)
                A("dve", "tensor_tensor", [rub_, rsg], [r_actt[cl][tb]], out=ACTT[:, cl, tbs(tb)], in0=ub_[:, :], in1=sg[:, :], op=ALU.mult)

            first = True
            for (c0, ncp) in parts:
                for cg in range(ncp // 2):
                    gs = gi % 2
                    gi += 1
                    cc = c0 + cg * 2
                    wdma(GUW[gs][:, 0, :, :], wcols("w_gu", l, cc * 128, 256), r_guw[gs], f"g{gs}")
                    wdma(GUW[gs][:, 1, :, :], wcols("w_gu", l, DFF + cc * 128, 256), r_guw[gs], f"g{gs}")
                    for ci in range(2):
                        cl = cg * 2 + ci
                        if first:
                            first = False
                            rmsnorm_T(PV_GFFN + l * 8, stat=None, after_tb=lambda tb, gs=gs, ci=ci, cl=cl: ffn_tile(gs, ci, cl, tb))
                            continue
                        for tb in range(4):
                            ffn_tile(gs, ci, cl, tb)
                for mp in range(4):
                    s = mp % 2
                    wd = wsv(s, 0, ncp, 256)
                    wdma(wd, W["w_down"][l][c0 * 128:(c0 + ncp) * 128, :].rearrange("(k p) n -> p k n", p=128)[:, :, mp * 256:(mp + 1) * 256],
                         r_ws[s], f"w{s}")
                    for mi in range(2):
                        m = mp * 2 + mi
                        for tb in range(4):
                            db, rdb = all_banks.next()
                            for k in range(ncp):
                                mm(db[:, :], wd[:, k, mi * 128:(mi + 1) * 128], ACTT[:, k, tbs(tb)], k == 0, k == ncp - 1, [r_ws[s], r_actt[k][tb]], rdb)
                            resid_add(db, rdb, m, tb)
            if dbg == "h3" and l == 0:
                for k in range(8):
                    dump(hT[:, k, :], r_hT[k], k)
            S.barrier()

        for l in range(nl):
            layer(l)
            S.barrier()

        lasts = {}
        for i in range(16):
            xs, rxs = XS[i % 2], r_xs[i % 2]
            for g in range(2):
                pb, rpb = all_banks.next()
                for kk in range(4):
                    k = g * 4 + kk
                    A("pe", "transpose", [r_hT[k][i // 4], r_const], [rpb], out=pb[:, kk * 128:(kk + 1) * 128], in_=hT[:, k, i * 128:(i + 1) * 128], identity=idf[:, :])
                if g == 0:
                    A("dve", "tensor_copy", [rpb], [rxs], out=xs[:, g * 512:(g + 1) * 512], in_=pb[:, :])
                else:
                    A("act", "activation", [rpb], [rxs], out=xs[:, g * 512:(g + 1) * 512], in_=pb[:, :], func=AF.Copy)
            lasts[i % 2] = DMA("sp", f"s_out{i % 2}", [rxs], [], out=out_d[i * 128:(i + 1) * 128, :], in_=xs[:, :])
        fw = list(lasts.values()) + [o for k, o in S.last_dma.items() if k.startswith("s_dbg")]
        S.emit(nc, es, final_waits=fw)
        build.stats = {e: len(S.ops[e]) for e in ENGS}
        build.counts = dict(S.counts)
    return nc


def make_in_maps(inputs):
    cbf, idf = _host_consts()
    pvec = _host_pvec(inputs)
    x = np.ascontiguousarray(inputs["x"], dtype=np.float32)
    mem = np.ascontiguousarray(inputs["mem"], dtype=np.float32)
    ws = {n: np.ascontiguousarray(inputs[n], dtype=np.float32) for n in WNAMES}
    maps = []
    for b in range(x.shape[0]):
        m = {"x": x[b], "mem": mem[b], "cbf": cbf, "idf": idf, "pvec": pvec}
        m.update(ws)
        maps.append(m)
    return maps


def kernel(**inputs):
    inputs = {k: np.asarray(v) for k, v in inputs.items()}
    nc = build()
    maps = make_in_maps(inputs)
    res = run_bass_kernel_spmd(nc, maps, core_ids=list(range(NB)))
    out = np.stack([np.asarray(r["out"], dtype=np.float32) for r in res.results], axis=0)
    return out
```

```python
import numpy as np
import ml_dtypes
from contextlib import ExitStack
import concourse.bass as bass
import concourse.mybir as mybir
from concourse.bass_utils import run_bass_kernel_spmd

F32 = mybir.dt.float32
BF16 = mybir.dt.bfloat16
AF = mybir.ActivationFunctionType
ALU = mybir.AluOpType

D = 1024
T = 2048
DEPTH = 4
NB = 8
MEMT = 256
DFF = 2816
INW = 5128
EPS = 1e-6
NEG = -30000.0

ENGS = ("pe", "act", "dve", "pool", "sp")


class Res:
    __slots__ = ("w", "r")

    def __init__(self):
        self.w = None
        self.r = {}


class Op:
    __slots__ = ("eng", "fn", "deps", "signal", "dma_sem", "val")

    def __init__(self, eng, fn, dma_sem):
        self.eng = eng
        self.fn = fn
        self.deps = []
        self.signal = False
        self.dma_sem = dma_sem
        self.val = None

    def key(self):
        return self.dma_sem if self.dma_sem is not None else self.eng


class Sched:
    def __init__(self, same_engine_sync=True):
        self.ops = {e: [] for e in ENGS}
        self.same_engine_sync = same_engine_sync
        self.dma_keys = []
        self.last_real = {}
        self.last_dma = {}
        self.dma_eng = {}

    def op(self, eng, fn, reads=(), writes=(), dma_sem=None):
        o = Op(eng, fn, dma_sem)
        if dma_sem is not None:
            if dma_sem not in self.dma_keys:
                self.dma_keys.append(dma_sem)
                self.dma_eng[dma_sem] = eng
            assert self.dma_eng[dma_sem] == eng
        deps = {}
        for r in reads:
            if r.w is not None:
                deps[id(r.w)] = r.w
        for w in writes:
            if w.w is not None:
                deps[id(w.w)] = w.w
            for rd in w.r.values():
                deps[id(rd)] = rd
        for d in deps.values():
            if d.dma_sem is None and o.dma_sem is None and d.eng == o.eng:
                if d.eng == "pe" or not self.same_engine_sync:
                    continue
            if d.dma_sem is not None and d.dma_sem == o.dma_sem:
                continue
            o.deps.append(d)
            d.signal = True
        for r in reads:
            r.r[o.key()] = o
        for w in writes:
            w.w = o
            w.r = {}
        self.ops[eng].append(o)
        if dma_sem is not None:
            self.last_dma[dma_sem] = o
        else:
            self.last_real[eng] = o
        return o

    def barrier(self, engines=("pe", "act", "dve", "sp"), dma_prefixes=("s_",)):
        lasts = [(e, self.last_real[e]) for e in engines if e in self.last_real]
        dmas = [o for k, o in self.last_dma.items() if k.startswith(dma_prefixes)]
        for e in engines:
            o = Op(e, None, None)
            for e2, l in lasts:
                if e2 != e:
                    o.deps.append(l)
                    l.signal = True
            for d in dmas:
                o.deps.append(d)
            self.ops[e].append(o)

    def emit(self, nc, es, final_waits=()):
        sems = {}
        for e in ENGS:
            sems[e] = es.enter_context(nc.semaphore("sem_" + e))
        for k in self.dma_keys:
            sems[k] = es.enter_context(nc.semaphore("dsem_" + k))
        cnt = {}
        for e in ENGS:
            for o in self.ops[e]:
                k = o.key()
                if o.fn is None:
                    continue
                if o.dma_sem is not None:
                    cnt[k] = cnt.get(k, 0) + 16
                    o.val = cnt[k]
                elif o.signal:
                    cnt[k] = cnt.get(k, 0) + 1
                    o.val = cnt[k]
        self.counts = cnt
        block = es.enter_context(nc.Block())
        engobj = {"pe": "tensor", "act": "scalar", "dve": "vector", "pool": "gpsimd", "sp": "sync"}

        def make(e):
            def body(eng):
                waited = {}
                for o in self.ops[e]:
                    for d in o.deps:
                        k = d.key()
                        if waited.get(k, 0) >= d.val:
                            continue
                        eng.wait_ge(sems[k], d.val)
                        waited[k] = d.val
                    if o.fn is None:
                        continue
                    ins = getattr(eng, o.fn[0])(**o.fn[1])
                    if o.dma_sem is not None:
                        ins.then_inc(sems[o.dma_sem], 16)
                    elif o.signal:
                        ins.then_inc(sems[e], 1)
                if e == "sp":
                    for d in final_waits:
                        eng.wait_ge(sems[d.key()], d.val)
            return body

        for e in ENGS:
            getattr(block, engobj[e])(make(e))


class Rot:
    def __init__(self, items):
        self.items = items
        self.i = 0

    def next(self):
        it = self.items[self.i % len(self.items)]
        self.i += 1
        return it


def _host_consts():
    cb = np.zeros((128, 30 * 128), np.float32)
    cb[:, 0:128] = np.eye(128)
    s = np.arange(128)[:, None]
    t = np.arange(128)[None, :]
    cb[:, 128:256] = np.where(s <= t, 0.0, NEG)
    bd = np.zeros((128, 128))
    bd[:64, :64] = 1.0 / 64
    bd[64:, 64:] = 1.0 / 64
    cb[:, 256:384] = bd
    cb[:, 384:512] = 1.0 / 128
    cb[:, 512:640] = 1.0 / 1024
    cb[:, 640:768] = 1.0
    for h in range(8):
        base = 64 if h % 2 == 0 else 0
        sq = np.zeros((128, 128))
        sk = np.zeros((128, 128))
        for j, r in enumerate((h, 32 + h, 64 + h)):
            sq[r, base + j] = -1.0
            sk[r, base + 3 + j] = 1.0
        for j in range(3):
            sq[96, base + 3 + j] = 1.0
            sk[96, base + j] = 1.0
        cb[:, 768 + h * 128: 768 + (h + 1) * 128] = sq
        cb[:, 768 + (8 + h) * 128: 768 + (9 + h) * 128] = sk
        pr = h // 2
        cb[:, 2816 + pr * 128: 2816 + (pr + 1) * 128] += sq
        cb[:, 2816 + (4 + pr) * 128: 2816 + (5 + pr) * 128] += sk
    return cb.astype(ml_dtypes.bfloat16), np.eye(128, dtype=np.float32)


PV_GMIX, PV_GMQ, PV_GMKV, PV_GFFN = 0, 32, 64, 96
PV_QG, PV_KG, PV_CQG, PV_CKG = 128, 132, 136, 140
PV_CONV = 144
PV_BF = 192
PV_N = 196


def _host_pvec(inp):
    pv = np.zeros((128, PV_N), np.float32)
    p = np.arange(128)
    for l in range(DEPTH):
        for k in range(8):
            pv[:, PV_GMIX + l * 8 + k] = inp["norm_mix"][l, k * 128 + p]
            pv[:, PV_GMQ + l * 8 + k] = inp["norm_mem_q"][l, k * 128 + p]
            pv[:, PV_GMKV + l * 8 + k] = inp["norm_mem_kv"][l, k * 128 + p]
            pv[:, PV_GFFN + l * 8 + k] = inp["norm_ffn"][l, k * 128 + p]
        pv[:, PV_QG + l] = inp["q_gain"][l, p % 64]
        pv[:, PV_KG + l] = inp["k_gain"][l, p % 64]
        pv[:, PV_CQG + l] = inp["cq_gain"][l, p]
        pv[:, PV_CKG + l] = inp["ck_gain"][l, p]
        for i in range(3):
            for c in range(4):
                pv[:, PV_CONV + (l * 3 + i) * 4 + c] = inp["conv_w"][l, i, c * 128 + p]
        bf = np.zeros(128, np.float32)
        for base in (0, 32, 64):
            bf[base:base + 8] = inp["b_f"][l]
        pv[:, PV_BF + l] = bf
    return pv


WNAMES = ["w_in", "w_up_a", "w_up_b", "w_o", "w_cq", "w_ckv", "w_co", "w_gu", "w_down"]
WSHAPES = {"w_in": [DEPTH, D, INW], "w_up_a": [DEPTH, 512, D], "w_up_b": [DEPTH, 512, D], "w_o": [DEPTH, D, D],
           "w_cq": [DEPTH, D, 512], "w_ckv": [DEPTH, D, 1024], "w_co": [DEPTH, 512, D],
           "w_gu": [DEPTH, D, 2 * DFF], "w_down": [DEPTH, DFF, D]}


def build(nl=DEPTH, dbg=None, ses=True, stop=None):
    nc = bass.Bass("TRN2", target_bir_lowering=False)
    x_d = nc.dram_tensor("x", [T, D], F32, kind="ExternalInput").ap()
    mem_d = nc.dram_tensor("mem", [MEMT, D], F32, kind="ExternalInput").ap()
    cbf_d = nc.dram_tensor("cbf", [128, 30 * 128], BF16, kind="ExternalInput").ap()
    idf_d = nc.dram_tensor("idf", [128, 128], F32, kind="ExternalInput").ap()
    pv_d = nc.dram_tensor("pvec", [128, PV_N], F32, kind="ExternalInput").ap()
    W = {n: nc.dram_tensor(n, WSHAPES[n], F32, kind="ExternalInput").ap() for n in WNAMES}
    out_d = nc.dram_tensor("out", [T, D], F32, kind="ExternalOutput").ap()
    dbg_d = None
    if dbg is not None:
        dbg_d = nc.dram_tensor("dbg", [128, 8, 2048], F32, kind="ExternalOutput").ap()

    es = ExitStack()
    with es:
        def sb(name, shape, dt):
            return es.enter_context(nc.sbuf_tensor(name, shape, dt))

        hT = sb("hT", [128, 8, T], F32)
        uT = sb("uT", [128, 8, T], BF16)
        RB = sb("RB", [128, 36864], BF16)
        WS = sb("WS", [128, 2, 3072], BF16)
        cbf = sb("cbf_s", [128, 30 * 128], BF16)
        idf = sb("idf_s", [128, 128], F32)
        pv = sb("pv_s", [128, PV_N], F32)
        pv2 = sb("pv2_s", [128, 16], F32)
        wfs = sb("wfs", [128, 8, 8], BF16)
        wfr = sb("wfr", [128, 8, 96], BF16)
        sqp_t = [sb(f"sq{i}", [128, 512], BF16) for i in range(2)]
        sq3_t = [sb(f"sq3_{i}", [128, 512], BF16) for i in range(2)]
        f32_t = [sb(f"f32_{i}", [128, 512], F32) for i in range(4)]
        pt_t = [sb(f"pt{i}", [128, 512], BF16) for i in range(4)]
        sm_t = sb("small", [128, 16], F32)
        ps_t = [es.enter_context(nc.psum_tensor(f"ps{i}", [128, 512], F32)) for i in range(8)]

        S = Sched(same_engine_sync=ses)

        def A(eng, meth, reads, writes, **kw):
            return S.op(eng, (meth, kw), reads=reads, writes=writes)

        def DMA(eng, key, reads, writes, **kw):
            return S.op(eng, ("dma_start", kw), reads=reads, writes=writes, dma_sem=key)

        sqp = Rot([(t, Res()) for t in sqp_t])
        sq3p = Rot([(t, Res()) for t in sq3_t])
        f32p = Rot([(t, Res()) for t in f32_t])
        ptp = Rot([(t, Res()) for t in pt_t])
        r_ps = [Res() for _ in range(8)]
        gen_banks = Rot([(ps_t[i], r_ps[i]) for i in range(6)])
        o_banks = Rot([(ps_t[i], r_ps[i]) for i in (6, 7)])
        all_banks = Rot([(ps_t[i], r_ps[i]) for i in range(8)])
        r_ws = [Res(), Res()]
        r_hT = [[Res() for _ in range(4)] for _ in range(8)]
        r_uT = [[Res() for _ in range(4)] for _ in range(8)]
        r_const = Res()
        r_pv2 = Res()
        r_wfs, r_wfr = Res(), Res()
        r_small = Res()
        r_carry = Res()

        IDENT = cbf[:, 0:128]
        MASKB = cbf[:, 128:256]
        BD64 = cbf[:, 256:384]
        ONES128 = cbf[:, 384:512]
        ONESD = cbf[:, 512:640]
        ONE1 = cbf[:, 640:768]

        def SELQ(h):
            return cbf[:, 768 + h * 128: 768 + (h + 1) * 128]

        def SELK(h):
            return cbf[:, 768 + (8 + h) * 128: 768 + (9 + h) * 128]

        def pcol(c):
            return pv[:, c:c + 1]

        def tbs(tb):
            return slice(tb * 512, (tb + 1) * 512)

        def rbv(off, n):
            return RB[:, off:off + n]

        VAUG = [rbv(s * 3072, 3072).rearrange("p (i c) -> p i c", c=192) for s in range(2)]
        QK = [rbv(6144 + i * 2048, 2048) for i in range(4)]
        QKB = [QK, [rbv(28672 + i * 2048, 2048) for i in range(4)]]
        CF = rbv(14336, 4096).bitcast(F32)
        CTAB = rbv(18432, 2048)
        AOUT = rbv(20480, 8192).rearrange("p (c t) -> p c t", c=4)
        COUT = rbv(28672, 8192).rearrange("p (c t) -> p c t", c=4)
        XB = rbv(0, 4104).bitcast(F32)
        MERGED = rbv(0, 16384).rearrange("p (c t) -> p c t", c=8)
        QMT = rbv(0, 8192).rearrange("p (c t) -> p c t", c=4)
        OMT = rbv(8192, 8192).rearrange("p (c t) -> p c t", c=4)
        MSTG = rbv(16384, 4096).bitcast(F32).rearrange("p (i d) -> p i d", i=2)
        MJUNK = rbv(20480, 1024)
        MEMNT = rbv(28672, 2048).rearrange("p (k t) -> p k t", k=8)
        KMT = rbv(30720, 1024).rearrange("p (h t) -> p h t", h=4)
        VM = rbv(31744, 1024).rearrange("p (i c) -> p i c", i=2)
        ACTT = rbv(0, 16384).rearrange("p (c t) -> p c t", c=8)
        GUW = [rbv(16384 + s * 4096, 4096).rearrange("p (g k n) -> p g k n", g=2, k=8) for s in range(3)]
        XS = [rbv(i * 2048, 2048).bitcast(F32) for i in range(2)]

        r_vaug = [[Res() for _ in range(4)] for _ in range(2)]
        r_qk = [[[Res(), Res()] for _ in range(4)] for _ in range(4)]
        r_qkb = [r_qk, [[[Res(), Res()] for _ in range(4)] for _ in range(4)]]
        r_cf = [Res() for _ in range(4)]
        r_ctab = [Res() for _ in range(4)]
        r_aout = [[[Res(), Res()] for _ in range(4)] for _ in range(4)]
        r_cout = [[Res() for _ in range(4)] for _ in range(4)]
        r_xb = [Res() for _ in range(5)]
        r_merged = [[Res() for _ in range(4)] for _ in range(8)]
        r_qmt = [[Res() for _ in range(4)] for _ in range(4)]
        r_omt = [[Res() for _ in range(4)] for _ in range(4)]
        r_mstg, r_mjunk, r_memnt = Res(), Res(), Res()
        r_kmt = [Res() for _ in range(4)]
        r_vm = Res()
        r_actt = [[Res() for _ in range(4)] for _ in range(8)]
        r_guw = [Res() for _ in range(3)]
        r_xs = [Res(), Res()]

        pool_dmas = []

        def wdma(dst, src, res, key):
            o = DMA("pool", key, [], [res], out=dst, in_=src, max_dma_last_dim=2048)
            pool_dmas.append(o)
            return o

        def wcols(name, l, c0, n):
            return W[name][l].rearrange("(k p) n -> p k n", p=128)[:, :, c0:c0 + n]

        def wsv(s, off, kc, n):
            return WS[:, s, off:off + kc * n].rearrange("p (k n) -> p k n", k=kc)

        def mm(ps, lhsT, rhs, start, stop, reads, rps):
            A("pe", "matmul", reads, [rps], out=ps, lhsT=lhsT, rhs=rhs, start=start, stop=stop)

        def proj(ps, rps, wv, act_fn, nk, wres, ares_fn):
            for k in range(nk):
                mm(ps, wv[:, k, :], act_fn(k), k == 0, k == nk - 1, [wres] + list(ares_fn(k)), rps)

        def dump(ap, res, slot, is_bf16=False, np_=128):
            if dbg_d is None:
                return None
            n = ap.shape[-1]
            dst = dbg_d[0:np_, slot, 0:n]
            if is_bf16:
                return DMA("pool", "s_dbgp", list(res), [], out=dst, in_=ap, max_dma_last_dim=1024)
            return DMA("sp", "s_dbg", list(res), [], out=dst, in_=ap)

        def rmsnorm_T(gbase, stat=None, after_tb=None):
            for tb in range(4):
                msb, rms = stat if stat is not None else gen_banks.next()
                for k in range(8):
                    sq, rsq = sqp.next()
                    A("act", "activation", [r_hT[k][tb]], [rsq], out=sq[:, :], in_=hT[:, k, tbs(tb)], func=AF.Square)
                    mm(msb[:, :], ONESD, sq[:, :], k == 0, k == 7, [rsq, r_const], rms)
                rstd, rr = f32p.next()
                A("act", "activation", [rms, r_pv2], [rr], out=rstd[:, :], in_=msb[:, :], func=AF.Ln, bias=pv2[:, 12:13], scale=1.0)
                A("act", "activation", [rr], [rr], out=rstd[:, :], in_=rstd[:, :], func=AF.Exp, scale=-0.5)
                for k in range(8):
                    A("dve", "scalar_tensor_tensor", [r_hT[k][tb], rr, r_const], [r_uT[k][tb]],
                      out=uT[:, k, tbs(tb)], in0=hT[:, k, tbs(tb)], scalar=pcol(gbase + k), in1=rstd[:, :],
                      op0=ALU.mult, op1=ALU.mult)
                if after_tb is not None:
                    after_tb(tb)

        def resid_add(ps, rps, m, tb):
            A("dve", "tensor_tensor", [rps, r_hT[m][tb]], [r_hT[m][tb]],
              out=hT[:, m, tbs(tb)], in0=ps[:, :], in1=hT[:, m, tbs(tb)], op=ALU.add)

        def qknorm(pb, rpb, N, statmat, banks):
            sq, rsq = sqp.next()
            A("act", "activation", [rpb], [rsq], out=sq[:, 0:N], in_=pb[:, 0:N], func=AF.Square)
            msb, rms = banks.next()
            mm(msb[:, 0:N], statmat, sq[:, 0:N], True, True, [rsq, r_const], rms)
            rstd, rr = f32p.next()
            A("act", "activation", [rms, r_pv2], [rr], out=rstd[:, 0:N], in_=msb[:, 0:N], func=AF.Ln, bias=pv2[:, 12:13], scale=1.0)
            A("act", "activation", [rr], [rr], out=rstd[:, 0:N], in_=rstd[:, 0:N], func=AF.Exp, scale=-0.5)
            return rstd, rr

        SK = set()
        if "cbf" not in SK:
            DMA("sp", "s_c", [], [r_const], out=cbf[:, :], in_=cbf_d[:, :])
        DMA("sp", "s_c", [], [r_const], out=idf[:, :], in_=idf_d[:, :])
        if "pv" not in SK:
            DMA("sp", "s_c", [], [r_const], out=pv[:, :], in_=pv_d[:, :])
        if "pv2" not in SK:
          A("dve", "tensor_scalar", [r_const], [r_pv2], out=pv2[:, 0:4], in0=pv[:, PV_QG:PV_QG + 4], scalar1=0.125, scalar2=None, op0=ALU.mult)
        if "pv2" not in SK:
          A("dve", "tensor_scalar", [r_const], [r_pv2], out=pv2[:, 4:8], in0=pv[:, PV_CQG:PV_CQG + 4], scalar1=float(128 ** -0.5), scalar2=None, op0=ALU.mult)
        if "pv2" not in SK:
          A("dve", "tensor_scalar", [r_const], [r_pv2], out=pv2[:, 8:12], in0=pv[:, PV_BF:PV_BF + 4], scalar1=-1.0, scalar2=None, op0=ALU.mult)
        if "ms" not in SK:
            A("dve", "memset", [], [r_wfr], ap=wfr[:, :, :], constant=0.0)
        if "ms2" not in SK:
            A("dve", "memset", [r_pv2], [r_pv2], ap=pv2[:, 12:13], constant=EPS)
            A("dve", "memset", [r_pv2], [r_pv2], ap=pv2[:, 13:14], constant=1.0)

        for i in range(16):
            xs, rxs = XS[i % 2], r_xs[i % 2]
            DMA("sp", f"s_x{i % 2}", [], [rxs], out=xs[:, :], in_=x_d[i * 128:(i + 1) * 128, :])
            for g in range(2):
                pb, rpb = all_banks.next()
                for kk in range(4):
                    k = g * 4 + kk
                    A("pe", "transpose", [rxs, r_const], [rpb], out=pb[:, kk * 128:(kk + 1) * 128], in_=xs[:, k * 128:(k + 1) * 128], identity=idf[:, :])
                tsl = slice(i * 128, (i + 1) * 128)
                wr = [r_hT[k_][i // 4] for k_ in range(g * 4, g * 4 + 4)]
                src = pb[:, :].rearrange("p (a b) -> p a b", b=128)
                if g == 0:
                    A("dve", "tensor_copy", [rpb], wr, out=hT[:, g * 4:(g + 1) * 4, tsl], in_=src)
                else:
                    A("act", "activation", [rpb], wr, out=hT[:, g * 4:(g + 1) * 4, tsl], in_=src, func=AF.Copy)
        S.barrier()

        def layer(l):
            A("dve", "memset", [], r_ctab, ap=CTAB[64:128, :], constant=1.0)
            for s in range(2):
                A("dve", "memset", [], r_vaug[s], ap=VAUG[s][:, :, 64:128], constant=1.0)
            wdma(wfs[:, :, :], wcols("w_in", l, 1536, 8), r_wfs, "wf")
            for base in (0, 32, 64):
                A("act", "activation", [r_wfs], [r_wfr], out=wfr[:, :, base:base + 8], in_=wfs[:, :, :], func=AF.Copy)

            def f_path(tb):
                fb, rfb = ps_t[1], r_ps[1]
                proj(fb[0:96, :], rfb, wfr, lambda k: uT[:, k, tbs(tb)], 8, r_wfr, lambda k: [r_uT[k][tb]])
                t, rt = f32p.next()
                A("act", "activation", [rfb, r_pv2], [rt], out=t[0:96, :], in_=fb[0:96, :], func=AF.Exp, bias=pv2[0:96, 8 + l:9 + l], scale=-1.0)
                A("act", "activation", [rt, r_pv2], [rt], out=t[0:96, :], in_=t[0:96, :], func=AF.Ln, bias=pv2[0:96, 13:14], scale=1.0)
                init = 0.0 if tb == 0 else sm_t[0:96, 8 + tb - 1:8 + tb]
                A("dve", "tensor_tensor_scan", [rt, r_carry], [r_cf[tb]],
                  out=CF[0:96, tbs(tb)], data0=t[0:96, :], data1=t[0:96, :], initial=init, op0=ALU.add, op1=ALU.max)
                A("dve", "tensor_copy", [r_cf[tb]], [r_carry], out=sm_t[0:96, 8 + tb:9 + tb], in_=CF[0:96, tb * 512 + 511:tb * 512 + 512])
                sl = tbs(tb)
                rc, rt_ = [r_cf[tb]], [r_ctab[tb]]
                A("dve", "tensor_copy", rc, rt_, out=CTAB[0:96, sl], in_=CF[0:96, sl])
                A("dve", "tensor_tensor", rt_ + rc, rc, out=CF[0:96, sl], in0=CF[0:96, sl], in1=CTAB[0:96, sl], op=ALU.subtract)
                A("dve", "tensor_copy", rc, rt_, out=CTAB[0:64, sl], in_=CF[0:64, sl])
                A("dve", "tensor_tensor", rt_ + rc, rc, out=CF[0:64, sl], in0=CF[0:64, sl], in1=CTAB[0:64, sl], op=ALU.subtract)
                A("dve", "tensor_copy", rc, rt_, out=CTAB[0:32, sl], in_=CF[0:32, sl])

            SB_ = [(ps_t[i], r_ps[i]) for i in (0, 1, 2)]
            OBK = [(ps_t[i], r_ps[i]) for i in (3, 4)]
            PJB = Rot([(ps_t[i], r_ps[i]) for i in (5, 6, 7)])
            s_rot = Rot(SB_)
            o_rot = Rot(OBK)

            def qkv_units(pr):
                s = pr % 2
                qb = pr % 2
                QKb, rqkb = QKB[qb], r_qkb[qb]
                hA, hB = 2 * pr, 2 * pr + 1
                wq, wk, wv = wsv(s, 0, 8, 128), wsv(s, 1024, 8, 128), wsv(s, 2048, 8, 128)
                wdma(wq, wcols("w_in", l, pr * 128, 128), r_ws[s], f"w{s}")
                wdma(wk, wcols("w_in", l, 512 + pr * 128, 128), r_ws[s], f"w{s}")
                wdma(wv, wcols("w_in", l, 1024 + pr * 128, 128), r_ws[s], f"w{s}")
                units = []
                struct = {"ab": [], "aug": [], "v": []}
                for tb in range(4):
                    st = {}

                    def uA(which, wmat, tb=tb, st=st):
                        pb, rpb = PJB.next()
                        proj(pb[:, :], rpb, wmat, lambda k: uT[:, k, tbs(tb)], 8, r_ws[s], lambda k: [r_uT[k][tb]])
                        sq, rsq = sqp.next()
                        A("act", "activation", [rpb], [rsq], out=sq[:, :], in_=pb[:, :], func=AF.Square)
                        st[which] = (pb, rpb, sq, rsq)

                    def uB(which, gcol, ia, ib, tb=tb, st=st):
                        pb, rpb, sq, rsq = st[which]
                        msb, rms = PJB.next()
                        mm(msb[:, :], BD64, sq[:, :], True, True, [rsq, r_const], rms)
                        rstd, rr = f32p.next()
                        A("act", "activation", [rms, r_pv2], [rr], out=rstd[:, :], in_=msb[:, :], func=AF.Ln, bias=pv2[:, 12:13], scale=1.0)
                        A("act", "activation", [rr], [rr], out=rstd[:, :], in_=rstd[:, :], func=AF.Exp, scale=-0.5)
                        A("dve", "scalar_tensor_tensor", [rpb, rr, r_const, r_pv2], [rqkb[ia][tb][0]],
                          out=QKb[ia][0:64, tbs(tb)], in0=pb[0:64, :], scalar=gcol[0:64, :], in1=rstd[0:64, :], op0=ALU.mult, op1=ALU.mult)
                        A("dve", "scalar_tensor_tensor", [rpb, rr, r_const, r_pv2], [rqkb[ib][tb][0]],
                          out=QKb[ib][64:128, tbs(tb)], in0=pb[64:128, :], scalar=gcol[64:128, :], in1=rstd[64:128, :], op0=ALU.mult, op1=ALU.mult)

                    def uAug(sel, ita, itb, tb=tb):
                        ab, rab = PJB.next()
                        mm(ab[:, :], sel, CTAB[:, tbs(tb)], True, True, [r_const, r_ctab[tb]], rab)
                        A("dve", "tensor_copy", [rab], [rqkb[ita][tb][1]], out=QKb[ita][64:128, tbs(tb)], in_=ab[64:128, :])
                        A("dve", "tensor_copy", [rab], [rqkb[itb][tb][1]], out=QKb[itb][0:64, tbs(tb)], in_=ab[0:64, :])

                    selq = cbf[:, 2816 + pr * 128: 2816 + (pr + 1) * 128]
                    selk = cbf[:, 2816 + (4 + pr) * 128: 2816 + (5 + pr) * 128]
                    ua_q = lambda uA=uA: uA("q", wq)
                    ua_k = lambda uA=uA: uA("k", wk)
                    ub_q = lambda uB=uB: uB("q", pv2[:, l:l + 1], 0, 1)
                    ub_k = lambda uB=uB: uB("k", pcol(PV_KG + l), 2, 3)
                    ug_q = lambda uAug=uAug: uAug(selq, 0, 1)
                    ug_k = lambda uAug=uAug: uAug(selk, 2, 3)
                    units += [ua_q, ub_q, ug_q, ua_k, ub_k, ug_k]
                    struct["ab"].append([ua_q, ub_q, ua_k, ub_k])
                    struct["aug"].append([ug_q, ug_k])
                for g in range(4):
                    def uV(g=g):
                        vb, rvb = PJB.next()
                        for ii in range(4):
                            i = 4 * g + ii
                            for k in range(8):
                                mm(vb[:, ii * 128:(ii + 1) * 128], uT[:, k, i * 128:(i + 1) * 128], wv[:, k, :], k == 0, k == 7,
                                   [r_ws[s], r_uT[k][g]], rvb)
                        vb3 = vb[:, :].rearrange("p (a b) -> p a b", b=128)
                        A("dve", "tensor_copy", [rvb], [r_vaug[s][g]], out=VAUG[s][:, 4 * g:4 * g + 4, 0:64], in_=vb3[:, :, 0:64])
                        A("dve", "tensor_copy", [rvb], [r_vaug[s][g]], out=VAUG[s][:, 4 * g:4 * g + 4, 128:192], in_=vb3[:, :, 64:128])
                    units.append(uV)
                    struct["v"].append(uV)
                return units, struct

            def attn_steps(pr):
                s = pr % 2
                qb = pr % 2
                QKb, rqkb = QKB[qb], r_qkb[qb]
                for hh in range(2):
                    Qt, Kt = QKb[hh], QKb[2 + hh]
                    rq, rk = rqkb[hh], rqkb[2 + hh]
                    vlo = 0 if hh == 0 else 64
                    olo, dlo = (0, 64) if hh == 0 else (64, 0)
                    tiles = [(j, i) for j in range(4) for i in range(4 * j + 4)]
                    pend = []
                    obank = {}

                    def emit_S(j, i):
                        r = i - 4 * j
                        c0 = 128 * max(r, 0)
                        N = 512 - c0
                        sbk, rsb = s_rot.next()
                        q0 = j * 512 + c0
                        qreads = [rq[j][0], rq[j][1], rk[i // 4][0], rk[i // 4][1]]
                        mm(sbk[:, 0:N], Kt[:, i * 128:(i + 1) * 128], Qt[:, q0:(j + 1) * 512], True, r < 0, qreads, rsb)
                        if r >= 0:
                            mm(sbk[:, 0:128], IDENT, MASKB, False, True, [r_const], rsb)
                        pt, rpt = ptp.next()
                        A("act", "activation", [rsb], [rpt], out=pt[:, 0:N], in_=sbk[:, 0:N], func=AF.Exp)
                        return (j, i, c0, N, pt, rpt)

                    def emit_PV(j, i, c0, N, pt, rpt):
                        n = 4 * j + 4
                        if i == 0:
                            obank[j] = o_rot.next()
                        ob, rob = obank[j]
                        mm(ob[:, c0:512], VAUG[s][:, i, vlo:vlo + 128], pt[:, 0:N], i == 0, i == n - 1, [rpt, r_vaug[s][i // 4]], rob)
                        if i == n - 1:
                            def fin(ob=ob, rob=rob, j=j):
                                rec, rrec = f32p.next()
                                A("act", "activation", [rob], [rrec], out=rec[dlo:dlo + 64, :], in_=ob[dlo:dlo + 64, :], func=AF.Ln)
                                A("act", "activation", [rrec], [rrec], out=rec[dlo:dlo + 64, :], in_=rec[dlo:dlo + 64, :], func=AF.Exp, scale=-1.0)
                                A("dve", "tensor_tensor", [rob, rrec], [r_aout[pr][j][hh]],
                                  out=AOUT[olo:olo + 64, pr, tbs(j)], in0=ob[olo:olo + 64, :], in1=rec[dlo:dlo + 64, :], op=ALU.mult)
                            deferred.append([3, fin])

                    LA = 2
                    deferred = []

                    def tick():
                        for d in list(deferred):
                            d[0] -= 1
                            if d[0] <= 0:
                                deferred.remove(d)
                                d[1]()

                    for (j, i) in tiles:
                        pend.append(emit_S(j, i))
                        if len(pend) > LA:
                            emit_PV(*pend.pop(0))
                        tick()
                        yield
                    while pend:
                        emit_PV(*pend.pop(0))
                    while deferred:
                        tick()

            _, st0 = qkv_units(0)

            def after1(tb):
                f_path(tb)
                for u in st0["ab"][tb]:
                    u()
                st0["v"][tb]()
                for u in st0["aug"][tb]:
                    u()

            rmsnorm_T(PV_GMIX + l * 8, stat=(ps_t[0], r_ps[0]), after_tb=after1)
            for pr in range(4):
                nxt = qkv_units(pr + 1)[0] if pr < 3 else []
                cnt = 0
                for _ in attn_steps(pr):
                    cnt += 1
                    if cnt % 2 == 0 and nxt:
                        nxt.pop(0)()
                while nxt:
                    nxt.pop(0)()
            if dbg == "aout" and l == 0:
                for c in range(4):
                    dump(AOUT[:, c, :], [r_aout[c][j_][hf] for j_ in range(4) for hf in range(2)], c, is_bf16=True)
            S.barrier()

            if stop == "s5":
                return
            A("dve", "memset", [], [r_xb[0]], ap=XB[:, 0:2], constant=0.0)
            for c in range(4):
                s = c % 2
                wz, wgb, wgc = wsv(s, 0, 8, 128), wsv(s, 1024, 8, 128), wsv(s, 2048, 8, 128)
                wdma(wz, wcols("w_in", l, 1544 + c * 128, 128), r_ws[s], f"w{s}")
                wdma(wgb, wcols("w_in", l, 2056 + c * 128, 128), r_ws[s], f"w{s}")
                wdma(wgc, wcols("w_in", l, 2568 + c * 128, 128), r_ws[s], f"w{s}")
                w0, w1, w2 = (pcol(PV_CONV + (l * 3 + i) * 4 + c) for i in range(3))
                for tb in range(4):
                    ur = lambda k: [r_uT[k][tb]]
                    ua = lambda k: uT[:, k, tbs(tb)]
                    zb, rzb = gen_banks.next()
                    proj(zb[:, :], rzb, wz, ua, 8, r_ws[s], ur)
                    gcb, rgcb = gen_banks.next()
                    proj(gcb[:, :], rgcb, wgc, ua, 8, r_ws[s], ur)
                    gbb, rgbb = gen_banks.next()
                    proj(gbb[:, :], rgbb, wgb, ua, 8, r_ws[s], ur)
                    gcs, rgcs = f32p.next()
                    A("act", "activation", [rgcb], [rgcs], out=gcs[:, :], in_=gcb[:, :], func=AF.Copy)
                    A("dve", "tensor_tensor", [rzb, rgcs], [r_xb[tb + 1]], out=XB[:, 2 + tb * 512:2 + (tb + 1) * 512], in0=zb[:, :], in1=gcs[:, :], op=ALU.mult)
                    y, ry = f32p.next()
                    xr = [r_xb[tb], r_xb[tb + 1], r_const]
                    A("dve", "tensor_scalar", xr, [ry], out=y[:, :], in0=XB[:, tb * 512:tb * 512 + 512], scalar1=w0, scalar2=None, op0=ALU.mult)
                    A("dve", "scalar_tensor_tensor", xr + [ry], [ry], out=y[:, :], in0=XB[:, tb * 512 + 1:tb * 512 + 513], scalar=w1, in1=y[:, :], op0=ALU.mult, op1=ALU.add)
                    A("dve", "scalar_tensor_tensor", xr + [ry], [ry], out=y[:, :], in0=XB[:, tb * 512 + 2:tb * 512 + 514], scalar=w2, in1=y[:, :], op0=ALU.mult, op1=ALU.add)
                    A("dve", "tensor_tensor", [rgbb, ry], [r_cout[c][tb]], out=COUT[:, c, tbs(tb)], in0=gbb[:, :], in1=y[:, :], op=ALU.mult)
            if dbg == "cout" and l == 0:
                for c in range(4):
                    dump(COUT[:, c, :], r_cout[c], c, is_bf16=True)
            S.barrier()

            if stop == "s6":
                return
            DMA("sp", "s_m", [], [r_mstg], out=MSTG[:, :, :], in_=mem_d.rearrange("(i p) d -> p i d", p=128))
            for i in range(2):
                jt, rjt = f32p.next()
                A("act", "activation", [r_mstg], [rjt, r_small], out=jt[:, :].bitcast(BF16), in_=MSTG[:, i, :], func=AF.Square, accum_out=sm_t[:, i:i + 1])
            A("dve", "tensor_scalar", [r_small], [r_small], out=sm_t[:, 2:4], in0=sm_t[:, 0:2], scalar1=1.0 / D, scalar2=EPS, op0=ALU.mult, op1=ALU.add)
            A("act", "activation", [r_small], [r_small], out=sm_t[:, 4:6], in_=sm_t[:, 2:4], func=AF.Ln)
            A("act", "activation", [r_small], [r_small], out=sm_t[:, 4:6], in_=sm_t[:, 4:6], func=AF.Exp, scale=-0.5)
            for i in range(2):
                A("dve", "tensor_scalar", [r_small, r_mstg], [r_mstg], out=MSTG[:, i, :], in0=MSTG[:, i, :], scalar1=sm_t[:, 4 + i:5 + i], scalar2=None, op0=ALU.mult)
            for m in range(8):
                s = m % 2
                wga, wgb2, wua, wub = wsv(s, 0, 8, 128), wsv(s, 1024, 8, 128), wsv(s, 2048, 4, 128), wsv(s, 2560, 4, 128)
                wdma(wga, wcols("w_in", l, 3080 + m * 128, 128), r_ws[s], f"w{s}")
                wdma(wgb2, wcols("w_in", l, 4104 + m * 128, 128), r_ws[s], f"w{s}")
                wdma(wua, wcols("w_up_a", l, m * 128, 128), r_ws[s], f"w{s}")
                wdma(wub, wcols("w_up_b", l, m * 128, 128), r_ws[s], f"w{s}")
                for tb in range(4):
                    ur = lambda k: [r_uT[k][tb]]
                    ua = lambda k: uT[:, k, tbs(tb)]
                    gab, rgab = all_banks.next()
                    proj(gab[:, :], rgab, wga, ua, 8, r_ws[s], ur)
                    uab, ruab = all_banks.next()
                    proj(uab[:, :], ruab, wua, lambda k: AOUT[:, k, tbs(tb)], 4, r_ws[s], lambda k: r_aout[k][tb])
                    gbb, rgbb = all_banks.next()
                    proj(gbb[:, :], rgbb, wgb2, ua, 8, r_ws[s], ur)
                    ubb, rubb = all_banks.next()
                    proj(ubb[:, :], rubb, wub, lambda k: COUT[:, k, tbs(tb)], 4, r_ws[s], lambda k: [r_cout[k][tb]])
                    sa, rsa = f32p.next()
                    sb_, rsb_ = f32p.next()
                    A("act", "activation", [rgab], [rsa], out=sa[:, :], in_=gab[:, :], func=AF.Sigmoid)
                    A("act", "activation", [rgbb], [rsb_], out=sb_[:, :], in_=gbb[:, :], func=AF.Sigmoid)
                    A("dve", "tensor_tensor", [ruab, rsa], [rsa], out=sa[:, :], in0=uab[:, :], in1=sa[:, :], op=ALU.mult)
                    A("dve", "tensor_tensor", [rubb, rsb_], [rsb_], out=sb_[:, :], in0=ubb[:, :], in1=sb_[:, :], op=ALU.mult)
                    A("dve", "tensor_tensor", [rsa, rsb_], [r_merged[m][tb]], out=MERGED[:, m, tbs(tb)], in0=sa[:, :], in1=sb_[:, :], op=ALU.add)
            if dbg == "merged" and l == 0:
                for m in range(8):
                    dump(MERGED[:, m, :], r_merged[m], m, is_bf16=True)
            for mp in range(4):
                s = mp % 2
                wo = wsv(s, 0, 8, 256)
                wdma(wo, wcols("w_o", l, mp * 256, 256), r_ws[s], f"w{s}")
                for mi in range(2):
                    m = mp * 2 + mi
                    for tb in range(4):
                        db, rdb = all_banks.next()
                        for k in range(8):
                            mm(db[:, :], wo[:, k, mi * 128:(mi + 1) * 128], MERGED[:, k, tbs(tb)], k == 0, k == 7, [r_ws[s], r_merged[k][tb]], rdb)
                        resid_add(db, rdb, m, tb)
            if dbg == "h1" and l == 0:
                for k in range(8):
                    dump(hT[:, k, :], r_hT[k], k)
            S.barrier()

            if stop == "s7":
                return
            for i in range(2):
                for g in range(2):
                    pb, rpb = all_banks.next()
                    for kk in range(4):
                        k = g * 4 + kk
                        A("pe", "transpose", [r_mstg, r_const], [rpb], out=pb[:, kk * 128:(kk + 1) * 128], in_=MSTG[:, i, k * 128:(k + 1) * 128], identity=idf[:, :])
                    for kk in range(4):
                        k = g * 4 + kk
                        A("dve", "tensor_scalar", [rpb, r_const], [r_memnt], out=MEMNT[:, k, i * 128:(i + 1) * 128], in0=pb[:, kk * 128:(kk + 1) * 128],
                          scalar1=pcol(PV_GMKV + l * 8 + k), scalar2=None, op0=ALU.mult)
            for hp in range(2):
                s = hp % 2
                wkk = wsv(s, 0, 8, 256)
                wdma(wkk, wcols("w_ckv", l, hp * 256, 256), r_ws[s], f"w{s}")
                for hi_ in range(2):
                    h = hp * 2 + hi_
                    pb, rpb = all_banks.next()
                    for k in range(8):
                        mm(pb[:, 0:256], wkk[:, k, hi_ * 128:(hi_ + 1) * 128], MEMNT[:, k, :], k == 0, k == 7, [r_ws[s], r_memnt], rpb)
                    rstd, rr = qknorm(pb, rpb, 256, ONES128, all_banks)
                    A("dve", "scalar_tensor_tensor", [rpb, rr, r_const], [r_kmt[h]], out=KMT[:, h, :], in0=pb[:, 0:256], scalar=pcol(PV_CKG + l), in1=rstd[:, 0:256],
                      op0=ALU.mult, op1=ALU.mult)
            for vh in range(2):
                s = vh % 2
                wvv = wsv(s, 0, 8, 256)
                wdma(wvv, wcols("w_ckv", l, 512 + vh * 256, 256), r_ws[s], f"w{s}")
                for i in range(2):
                    pb, rpb = all_banks.next()
                    for k in range(8):
                        mm(pb[:, 0:256], MEMNT[:, k, i * 128:(i + 1) * 128], wvv[:, k, :], k == 0, k == 7, [r_ws[s], r_memnt], rpb)
                    A("act", "activation", [rpb], [r_vm], out=VM[:, i, vh * 256:(vh + 1) * 256], in_=pb[:, 0:256], func=AF.Copy)
            wqs = []
            for hp in range(2):
                wqq = wsv(hp, 0, 8, 256)
                wdma(wqq, wcols("w_cq", l, hp * 256, 256), r_ws[hp], f"w{hp}")
                wqs.append(wqq)
            PJ = [(ps_t[0], r_ps[0]), (ps_t[1], r_ps[1])]
            ST = (ps_t[2], r_ps[2])
            SC = [(ps_t[3], r_ps[3]), (ps_t[4], r_ps[4])]
            OB = [(ps_t[5], r_ps[5]), (ps_t[7], r_ps[7])]
            DN = (ps_t[6], r_ps[6])
            st3 = {}

            def p3_s1(n):
                h, tb = n // 4, n % 4
                pb, rpb = PJ[n % 2]
                wqq = wqs[h // 2]
                for k in range(8):
                    mm(pb[:, :], wqq[:, k, (h % 2) * 128:(h % 2 + 1) * 128], uT[:, k, tbs(tb)], k == 0, k == 7, [r_ws[h // 2], r_uT[k][tb]], rpb)
                sq, rsq = sq3p.next()
                A("act", "activation", [rpb], [rsq], out=sq[:, :], in_=pb[:, :], func=AF.Square)
                st3[n] = dict(pb=pb, rpb=rpb, sq=sq, rsq=rsq)

            def p3_s2(n):
                h, tb = n // 4, n % 4
                d = st3[n]
                msb, rms = ST
                mm(msb[:, :], ONES128, d["sq"][:, :], True, True, [d["rsq"], r_const], rms)
                rstd, rr = f32p.next()
                A("act", "activation", [rms, r_pv2], [rr], out=rstd[:, :], in_=msb[:, :], func=AF.Ln, bias=pv2[:, 12:13], scale=1.0)
                A("act", "activation", [rr], [rr], out=rstd[:, :], in_=rstd[:, :], func=AF.Exp, scale=-0.5)
                A("dve", "scalar_tensor_tensor", [d["rpb"], rr, r_pv2], [r_qmt[h][tb]], out=QMT[:, h, tbs(tb)], in0=d["pb"][:, :], scalar=pv2[:, 4 + l:5 + l], in1=rstd[:, :],
                  op0=ALU.mult, op1=ALU.mult)

            def p3_s3(n):
                h, tb = n // 4, n % 4
                pts = []
                for i in range(2):
                    sbk, rsb = SC[i]
                    mm(sbk[:, :], KMT[:, h, i * 128:(i + 1) * 128], QMT[:, h, tbs(tb)], True, True, [r_kmt[h], r_qmt[h][tb]], rsb)
                    pt, rpt = ptp.next()
                    A("act", "activation", [rsb], [rpt], out=pt[:, :], in_=sbk[:, :], func=AF.Exp)
                    pts.append((pt, rpt))
                st3[n]["pts"] = pts

            def p3_s4(n):
                h, tb = n // 4, n % 4
                pts = st3[n]["pts"]
                ob, rob = OB[n % 2]
                dbk, rdbk = DN
                for i, (pt, rpt) in enumerate(pts):
                    mm(dbk[:, :], ONE1, pt[:, :], i == 0, i == 1, [r_const, rpt], rdbk)
                for i, (pt, rpt) in enumerate(pts):
                    mm(ob[:, :], VM[:, i, h * 128:(h + 1) * 128], pt[:, :], i == 0, i == 1, [r_vm, rpt], rob)
                rec, rrec = f32p.next()
                if n % 2 == 0:
                    A("act", "activation", [rdbk], [rrec], out=rec[:, :], in_=dbk[:, :], func=AF.Ln)
                    A("act", "activation", [rrec], [rrec], out=rec[:, :], in_=rec[:, :], func=AF.Exp, scale=-1.0)
                else:
                    A("dve", "reciprocal", [rdbk], [rrec], out=rec[:, :], in_=dbk[:, :])
                A("dve", "tensor_tensor", [rob, rrec], [r_omt[h][tb]], out=OMT[:, h, tbs(tb)], in0=ob[:, :], in1=rec[:, :], op=ALU.mult)
                del st3[n]

            def p3_iter(it):
                if it < 16:
                    p3_s1(it)
                if 0 <= it - 1 < 16:
                    p3_s2(it - 1)
                if 0 <= it - 2 < 16:
                    p3_s3(it - 2)
                if 0 <= it - 3 < 16:
                    p3_s4(it - 3)

            rmsnorm_T(PV_GMQ + l * 8, stat=ST, after_tb=p3_iter)
            for it in range(4, 16 + 3):
                p3_iter(it)
            if dbg == "omt" and l == 0:
                for h in range(4):
                    dump(OMT[:, h, :], r_omt[h], h, is_bf16=True)
            for mp in range(4):
                s = mp % 2
                wco = wsv(s, 0, 4, 256)
                wdma(wco, wcols("w_co", l, mp * 256, 256), r_ws[s], f"w{s}")
                for mi in range(2):
                    m = mp * 2 + mi
                    for tb in range(4):
                        db, rdb = all_banks.next()
                        for k in range(4):
                            mm(db[:, :], wco[:, k, mi * 128:(mi + 1) * 128], OMT[:, k, tbs(tb)], k == 0, k == 3, [r_ws[s], r_omt[k][tb]], rdb)
                        resid_add(db, rdb, m, tb)
            if dbg == "h2" and l == 0:
                for k in range(8):
                    dump(hT[:, k, :], r_hT[k], k)
            S.barrier()

            if stop == "s8":
                return
            parts = [(0, 8), (8, 8), (16, 6)]
            gi = 0

            def ffn_tile(gs, ci, cl, tb):
                gb_, rgb_ = all_banks.next()
                for k in range(8):
                    mm(gb_[:, :], GUW[gs][:, 0, k, ci * 128:(ci + 1) * 128], uT[:, k, tbs(tb)], k == 0, k == 7, [r_guw[gs], r_uT[k][tb]], rgb_)
                ub_, rub_ = all_banks.next()
                for k in range(8):
                    mm(ub_[:, :], GUW[gs][:, 1, k, ci * 128:(ci + 1) * 128], uT[:, k, tbs(tb)], k == 0, k == 7, [r_guw[gs], r_uT[k][tb]], rub_)
                sg, rsg = f32p.next()
                A("act", "activation", [rgb_], [rsg], out=sg[:, :], in_=gb_[:, :], func=AF.Silu)
                A("dve", "tensor_tensor", [rub_, rsg], [r_actt[cl][tb]], out=ACTT[:, cl, tbs(tb)], in0=ub_[:, :], in1=sg[:, :], op=ALU.mult)

            first = True
            for (c0, ncp) in parts:
                for cg in range(ncp // 2):
                    gs = gi % 2
                    gi += 1
                    cc = c0 + cg * 2
                    wdma(GUW[gs][:, 0, :, :], wcols("w_gu", l, cc * 128, 256), r_guw[gs], f"g{gs}")
                    wdma(GUW[gs][:, 1, :, :], wcols("w_gu", l, DFF + cc * 128, 256), r_guw[gs], f"g{gs}")
                    for ci in range(2):
                        cl = cg * 2 + ci
                        if first:
                            first = False
                            rmsnorm_T(PV_GFFN + l * 8, stat=None, after_tb=lambda tb, gs=gs, ci=ci, cl=cl: ffn_tile(gs, ci, cl, tb))
                            continue
                        for tb in range(4):
                            ffn_tile(gs, ci, cl, tb)
                for mp in range(4):
                    s = mp % 2
                    wd = wsv(s, 0, ncp, 256)
                    wdma(wd, W["w_down"][l][c0 * 128:(c0 + ncp) * 128, :].rearrange("(k p) n -> p k n", p=128)[:, :, mp * 256:(mp + 1) * 256],
                         r_ws[s], f"w{s}")
                    for mi in range(2):
                        m = mp * 2 + mi
                        for tb in range(4):
                            db, rdb = all_banks.next()
                            for k in range(ncp):
                                mm(db[:, :], wd[:, k, mi * 128:(mi + 1) * 128], ACTT[:, k, tbs(tb)], k == 0, k == ncp - 1, [r_ws[s], r_actt[k][tb]], rdb)
                            resid_add(db, rdb, m, tb)
            if dbg == "h3" and l == 0:
                for k in range(8):
                    dump(hT[:, k, :], r_hT[k], k)
            S.barrier()

        for l in range(nl):
            layer(l)
            S.barrier()

        lasts = {}
        for i in range(16):
            xs, rxs = XS[i % 2], r_xs[i % 2]
            for g in range(2):
                pb, rpb = all_banks.next()
                for kk in range(4):
                    k = g * 4 + kk
                    A("pe", "transpose", [r_hT[k][i // 4], r_const], [rpb], out=pb[:, kk * 128:(kk + 1) * 128], in_=hT[:, k, i * 128:(i + 1) * 128], identity=idf[:, :])
                if g == 0:
                    A("dve", "tensor_copy", [rpb], [rxs], out=xs[:, g * 512:(g + 1) * 512], in_=pb[:, :])
                else:
                    A("act", "activation", [rpb], [rxs], out=xs[:, g * 512:(g + 1) * 512], in_=pb[:, :], func=AF.Copy)
            lasts[i % 2] = DMA("sp", f"s_out{i % 2}", [rxs], [], out=out_d[i * 128:(i + 1) * 128, :], in_=xs[:, :])
        fw = list(lasts.values()) + [o for k, o in S.last_dma.items() if k.startswith("s_dbg")]
        S.emit(nc, es, final_waits=fw)
        build.stats = {e: len(S.ops[e]) for e in ENGS}
        build.counts = dict(S.counts)
    return nc


def make_in_maps(inputs):
    cbf, idf = _host_consts()
    pvec = _host_pvec(inputs)
    x = np.ascontiguousarray(inputs["x"], dtype=np.float32)
    mem = np.ascontiguousarray(inputs["mem"], dtype=np.float32)
    ws = {n: np.ascontiguousarray(inputs[n], dtype=np.float32) for n in WNAMES}
    maps = []
    for b in range(x.shape[0]):
        m = {"x": x[b], "mem": mem[b], "cbf": cbf, "idf": idf, "pvec": pvec}
        m.update(ws)
        maps.append(m)
    return maps


def kernel(**inputs):
    inputs = {k: np.asarray(v) for k, v in inputs.items()}
    nc = build()
    maps = make_in_maps(inputs)
    res = run_bass_kernel_spmd(nc, maps, core_ids=list(range(NB)))
    out = np.stack([np.asarray(r["out"], dtype=np.float32) for r in res.results], axis=0)
    return out
```
